# Optimizing a Trainium2 kernel written in Bass

```python
import math
import jax, jax.numpy as jnp
from jax import lax
import numpy as np

D_MODEL = 1024
BATCH = 8
SEQ = 2048
DEPTH = 2

PLE_DIM = 256
HEAD_DIM = 64
DA_HEADS = 4
DA_V = 2 * HEAD_DIM
DA_QK = DA_HEADS * 2 * HEAD_DIM
DA_WIDTH = DA_HEADS * DA_V
MLA_HEADS = 8
MLA_NOPE = 64
MLA_ROPE = 32
MLA_V = 64
MLA_Q_RANK = 256
MLA_KV_RANK = 128
MLA_WIDTH = MLA_HEADS * MLA_V
SW_HEADS = 8
SW_KV_HEADS = 2
SW_GROUP = SW_HEADS // SW_KV_HEADS
SW_WINDOW = 128
SW_WIDTH = SW_HEADS * HEAD_DIM
N_BRANCHES = 3
REL_BUCKETS = 32
REL_MAX_DIST = 128
BIAS_HEADS = DA_HEADS + SW_HEADS
Q_BLOCK = 128
ROPE_THETA = 10000.0
EPS = 1e-6
NEG = -1e30

IN_SIZES = (DA_QK, DA_QK, DA_WIDTH, DA_WIDTH,
            MLA_Q_RANK, MLA_KV_RANK, MLA_ROPE, MLA_WIDTH,
            SW_HEADS * HEAD_DIM, SW_KV_HEADS * HEAD_DIM, SW_KV_HEADS * HEAD_DIM, SW_WIDTH,
            N_BRANCHES * D_MODEL)
IN_TOTAL = sum(IN_SIZES)

kernel_name = "hybrid_diffattn_mla_swa_gated_merge"


def rms_norm(x, w):
    xf = x.astype(jnp.float32)
    y = xf * lax.rsqrt(jnp.mean(xf * xf, axis=-1, keepdims=True) + EPS)
    return (y * w.astype(jnp.float32)).astype(x.dtype)


def rope(x, positions):
    half = x.shape[-1] // 2
    inv_freq = ROPE_THETA ** (-jnp.arange(half, dtype=jnp.float32) / half)
    ang = positions.astype(jnp.float32)[..., None] * inv_freq
    ang = ang.reshape(ang.shape[:2] + (1,) * (x.ndim - 3) + (half,))
    cos, sin = jnp.cos(ang), jnp.sin(ang)
    xf = x.astype(jnp.float32)
    x1, x2 = xf[..., :half], xf[..., half:]
    return jnp.concatenate([x1 * cos - x2 * sin, x2 * cos + x1 * sin], axis=-1).astype(x.dtype)


def t5_bucket(rel):
    n = jnp.maximum(rel, 0)
    max_exact = REL_BUCKETS // 2
    nf = jnp.maximum(n, 1).astype(jnp.float32)
    large = max_exact + (jnp.log(nf / max_exact) / math.log(REL_MAX_DIST / max_exact)
                         * (REL_BUCKETS - max_exact)).astype(jnp.int32)
    large = jnp.minimum(large, REL_BUCKETS - 1)
    return jnp.where(n < max_exact, n, large)


def sweep_query_blocks(block_fn, n_blocks):
    out = lax.map(block_fn, jnp.arange(n_blocks))
    out = jnp.moveaxis(out, 0, 1)
    return out.reshape((out.shape[0], n_blocks * Q_BLOCK) + out.shape[3:])


def diff_attention(q, k, v, lam, rel_table):
    S = q.shape[1]
    scale = HEAD_DIM ** -0.5
    kf = k.astype(jnp.float32)
    table = rel_table.astype(jnp.float32)
    k_idx = jnp.arange(S)

    def block(i):
        q0 = i * Q_BLOCK
        qb = lax.dynamic_slice_in_dim(q, q0, Q_BLOCK, axis=1).astype(jnp.float32)
        s = jnp.einsum('bqhmd,bkhmd->bhmqk', qb, kf) * scale
        rel = (q0 + jnp.arange(Q_BLOCK))[:, None] - k_idx[None, :]
        bias = jnp.moveaxis(table[t5_bucket(rel)][..., :DA_HEADS], -1, 0)
        s = jnp.where(rel >= 0, s + bias[None, :, None], NEG)
        pm = jax.nn.softmax(s, axis=-1)
        w = pm[:, :, 0] - lam * pm[:, :, 1]
        return jnp.einsum('bhqk,bkhe->bqhe', w.astype(v.dtype), v)

    return sweep_query_blocks(block, S // Q_BLOCK)


def mla_attention(q_nope, q_rope, k_nope, k_rope, v):
    S = q_nope.shape[1]
    scale = (MLA_NOPE + MLA_ROPE) ** -0.5
    kn = k_nope.astype(jnp.float32)
    kr = k_rope.astype(jnp.float32)
    k_idx = jnp.arange(S)

    def block(i):
        q0 = i * Q_BLOCK
        qn = lax.dynamic_slice_in_dim(q_nope, q0, Q_BLOCK, axis=1).astype(jnp.float32)
        qr = lax.dynamic_slice_in_dim(q_rope, q0, Q_BLOCK, axis=1).astype(jnp.float32)
        s = (jnp.einsum('bqhd,bkhd->bhqk', qn, kn)
             + jnp.einsum('bqhr,bkr->bhqk', qr, kr)) * scale
        rel = (q0 + jnp.arange(Q_BLOCK))[:, None] - k_idx[None, :]
        s = jnp.where(rel >= 0, s, NEG)
        pm = jax.nn.softmax(s, axis=-1)
        return jnp.einsum('bhqk,bkhd->bqhd', pm.astype(v.dtype), v)

    return sweep_query_blocks(block, S // Q_BLOCK)


def sliding_window_attention(q, k, v, sinks, rel_table):
    B, S = q.shape[:2]
    nb = S // Q_BLOCK
    scale = HEAD_DIM ** -0.5
    qb = q.reshape(B, nb, Q_BLOCK, SW_KV_HEADS, SW_GROUP, HEAD_DIM).astype(jnp.float32)

    def band(t):
        tp = jnp.pad(t, ((0, 0), (Q_BLOCK, 0), (0, 0), (0, 0)))
        tp = tp.reshape(B, nb + 1, Q_BLOCK, SW_KV_HEADS, HEAD_DIM)
        return jnp.concatenate([tp[:, :-1], tp[:, 1:]], axis=2)

    kb = band(k).astype(jnp.float32)
    vb = band(v)
    s = jnp.einsum('bnqhgd,bnkhd->bnhgqk', qb, kb) * scale
    qi = jnp.arange(Q_BLOCK)
    kj = jnp.arange(2 * Q_BLOCK)
    rel = Q_BLOCK + qi[:, None] - kj[None, :]
    bias = jnp.moveaxis(rel_table.astype(jnp.float32)[t5_bucket(rel)][..., DA_HEADS:], -1, 0)
    bias = bias.reshape(SW_KV_HEADS, SW_GROUP, Q_BLOCK, 2 * Q_BLOCK)
    in_window = (rel >= 0) & (rel < SW_WINDOW)
    key_valid = (jnp.arange(nb)[:, None] * Q_BLOCK + kj[None, :] - Q_BLOCK) >= 0
    mask = in_window[None] & key_valid[:, None, :]
    s = jnp.where(mask[None, :, None, None], s + bias, NEG)
    sink = jnp.broadcast_to(sinks.astype(jnp.float32).reshape(SW_KV_HEADS, SW_GROUP, 1, 1),
                            s.shape[:-1] + (1,))
    pm = jax.nn.softmax(jnp.concatenate([s, sink], axis=-1), axis=-1)[..., :-1]
    o = jnp.einsum('bnhgqk,bnkhd->bnqhgd', pm.astype(v.dtype), vb)
    return o.reshape(B, S, SW_WIDTH)


def setup_inputs(seed: int = 0) -> dict:
    key = jax.random.key(seed)
    ks = jax.random.split(key, 24)
    f32 = jnp.float32

    def nrm(k, shape, scale):
        return jax.random.normal(k, shape, f32) * scale

    def gain(k, shape):
        return 1.0 + 0.05 * jax.random.normal(k, shape, f32)

    start = jax.random.randint(ks[2], (BATCH, 1), 0, 4096, dtype=jnp.int32)
    positions = start + jnp.arange(SEQ, dtype=jnp.int32)[None, :]
    return {
        "x": nrm(ks[0], (BATCH, SEQ, D_MODEL), 1.0),
        "p": nrm(ks[1], (DEPTH, BATCH, SEQ, PLE_DIM), 1.0),
        "positions": positions,
        "rel_bias": nrm(ks[3], (REL_BUCKETS, BIAS_HEADS), 0.3),
        "norm_pre": gain(ks[4], (DEPTH, D_MODEL)),
        "norm_post": gain(ks[5], (DEPTH, D_MODEL)),
        "w_in": nrm(ks[6], (DEPTH, D_MODEL, IN_TOTAL), D_MODEL ** -0.5),
        "da_lambda": nrm(ks[7], (DEPTH, 4, HEAD_DIM), 0.1),
        "da_subln": gain(ks[8], (DEPTH, DA_V)),
        "mla_q_norm": gain(ks[9], (DEPTH, MLA_Q_RANK)),
        "mla_w_qb": nrm(ks[10], (DEPTH, MLA_Q_RANK, MLA_HEADS * (MLA_NOPE + MLA_ROPE)), MLA_Q_RANK ** -0.5),
        "mla_kv_norm": gain(ks[11], (DEPTH, MLA_KV_RANK)),
        "mla_w_kvb": nrm(ks[12], (DEPTH, MLA_KV_RANK, MLA_HEADS * (MLA_NOPE + MLA_V)), MLA_KV_RANK ** -0.5),
        "sw_sinks": nrm(ks[13], (DEPTH, SW_HEADS), 0.5),
        "w_br_a": nrm(ks[14], (DEPTH, DA_WIDTH, D_MODEL), DA_WIDTH ** -0.5),
        "w_br_b": nrm(ks[15], (DEPTH, MLA_WIDTH, D_MODEL), MLA_WIDTH ** -0.5),
        "w_br_c": nrm(ks[16], (DEPTH, SW_WIDTH, D_MODEL), SW_WIDTH ** -0.5),
        "w_out": nrm(ks[17], (DEPTH, D_MODEL, D_MODEL), D_MODEL ** -0.5),
        "w_ple_gate": nrm(ks[18], (DEPTH, D_MODEL, D_MODEL), D_MODEL ** -0.5),
        "w_ple_proj": nrm(ks[19], (DEPTH, PLE_DIM, D_MODEL), PLE_DIM ** -0.5),
    }


def reference(x, p, positions, rel_bias, norm_pre, norm_post, w_in, da_lambda, da_subln,
              mla_q_norm, mla_w_qb, mla_kv_norm, mla_w_kvb, sw_sinks,
              w_br_a, w_br_b, w_br_c, w_out, w_ple_gate, w_ple_proj):
    B, S, _ = x.shape
    split_points = tuple(int(c) for c in np.cumsum(IN_SIZES)[:-1])
    for l in range(DEPTH):
        h = rms_norm(x, norm_pre[l])
        u = h @ w_in[l]
        (a_q, a_k, a_v, a_z, b_q, b_kv, b_kr, b_z,
         c_q, c_k, c_v, c_z, g) = jnp.split(u, split_points, axis=-1)

        lam_init = 0.8 - 0.6 * math.exp(-0.3 * l)
        lf = da_lambda[l].astype(jnp.float32)
        lam = jnp.exp(jnp.sum(lf[0] * lf[1])) - jnp.exp(jnp.sum(lf[2] * lf[3])) + lam_init
        oa = diff_attention(a_q.reshape(B, S, DA_HEADS, 2, HEAD_DIM),
                            a_k.reshape(B, S, DA_HEADS, 2, HEAD_DIM),
                            a_v.reshape(B, S, DA_HEADS, DA_V), lam, rel_bias)
        oa = (rms_norm(oa, da_subln[l]) * (1.0 - lam_init)).reshape(B, S, DA_WIDTH)

        qm = (rms_norm(b_q, mla_q_norm[l]) @ mla_w_qb[l]).reshape(B, S, MLA_HEADS, MLA_NOPE + MLA_ROPE)
        q_nope, q_rope = qm[..., :MLA_NOPE], rope(qm[..., MLA_NOPE:], positions)
        kvm = (rms_norm(b_kv, mla_kv_norm[l]) @ mla_w_kvb[l]).reshape(B, S, MLA_HEADS, MLA_NOPE + MLA_V)
        k_nope, v_m = kvm[..., :MLA_NOPE], kvm[..., MLA_NOPE:]
        k_rope = rope(b_kr, positions)
        ob = mla_attention(q_nope, q_rope, k_nope, k_rope, v_m).reshape(B, S, MLA_WIDTH)

        oc = sliding_window_attention(c_q.reshape(B, S, SW_HEADS, HEAD_DIM),
                                      c_k.reshape(B, S, SW_KV_HEADS, HEAD_DIM),
                                      c_v.reshape(B, S, SW_KV_HEADS, HEAD_DIM),
                                      sw_sinks[l], rel_bias)

        gates = jax.nn.sigmoid(g).reshape(B, S, N_BRANCHES, D_MODEL)
        y = (gates[:, :, 0] * ((oa * jax.nn.silu(a_z)) @ w_br_a[l])
             + gates[:, :, 1] * ((ob * jax.nn.silu(b_z)) @ w_br_b[l])
             + gates[:, :, 2] * ((oc * jax.nn.silu(c_z)) @ w_br_c[l]))
        x = x + rms_norm(y @ w_out[l], norm_post[l])

        x = x + jax.nn.sigmoid(x @ w_ple_gate[l]) * (p[l] @ w_ple_proj[l])
    return x
```

```python
import math
import numpy as np
import ml_dtypes
import concourse.bass as bass
import concourse.mybir as mybir
from concourse.ap import AP
from concourse.bass_utils import run_bass_kernel_spmd

F32 = mybir.dt.float32
BF16 = mybir.dt.bfloat16
I32 = mybir.dt.int32
U8 = mybir.dt.uint8
AF = mybir.ActivationFunctionType
ALU = mybir.AluOpType

S_ = 2048
D_ = 1024
NT = 16
EPS = 1e-6
NEG = -30000.0
IN_TOTAL = 7328
OFF = dict(a_q=0, a_k=512, a_v=1024, a_z=1536, b_lat=2048, b_z=2464,
           c_q=2976, c_k=3488, c_v=3616, c_z=3744, g=4256)
ENGS = ('pe', 'act', 'dve', 'pool', 'sp')


class Op:
    __slots__ = ('eng', 'fn', 'deps', 'sem', 'tick', 'inc', 'stream', 'needed', 'idx')


class Sched:
    def __init__(self, nc):
        self.nc = nc
        self.order = {e: [] for e in ENGS}
        self.res = {}
        self.streams = {}
        self.n = 0

    def add(self, eng, fn, reads=(), writes=(), stream=None):
        op = Op()
        op.eng = eng; op.fn = fn; op.stream = stream; op.needed = False
        op.sem = None; op.tick = 0; op.inc = 0; op.idx = self.n; self.n += 1
        deps = set()
        for r in reads:
            st = self.res.setdefault(r, [[], [], []])
            deps.update(st[0])
            st[1].append(op)
        for w in writes:
            st = self.res.setdefault(w, [[], [], []])
            if st[1]:
                st[2] = st[1] + st[0]
                st[0] = []; st[1] = []
            deps.update(st[2])
            st[0].append(op)
        deps.discard(op)
        fixed = set()
        for d in deps:
            if d.stream is not None:
                d = self.streams[d.stream][-1]
            fixed.add(d)
        op.deps = fixed
        if stream is not None:
            self.streams.setdefault(stream, []).append(op)
        self.order[eng].append(op)
        return op

    def barrier(self):
        lasts = [self.order[e][-1] for e in ENGS if self.order[e]]
        lasts = [self.streams[d.stream][-1] if d.stream is not None else d for d in lasts]
        for s in self.streams.values():
            lasts.append(s[-1])
        for e in ENGS:
            op = self.add(e, None)
            op.deps = set(x for x in lasts)

    def finalize(self, sem_alloc):
        for e in ENGS:
            for op in self.order[e]:
                for d in op.deps:
                    d.needed = True
        self.eng_sem = {}
        scnt = {}
        self.stream_sem = {}
        for e in ENGS:
            cnt = 0
            sem = None
            for op in self.order[e]:
                if op.stream is not None:
                    if op.stream not in self.stream_sem:
                        self.stream_sem[op.stream] = sem_alloc('st_%s' % str(op.stream))
                    c = scnt.get(op.stream, 0) + 16
                    scnt[op.stream] = c
                    op.sem = self.stream_sem[op.stream]; op.tick = c; op.inc = 16
                elif op.needed and op.fn is not None:
                    if sem is None or cnt >= 30000:
                        sem = sem_alloc('eng_%s_%d' % (e, op.idx)); cnt = 0
                    cnt += 1
                    op.sem = sem; op.tick = cnt; op.inc = 1
                elif op.needed:
                    pass

    def emit(self, e, h):
        seen = {}

        def waits(op, depth=0):
            for d in sorted(op.deps, key=lambda o: o.idx):
                if d.fn is None and d.stream is None:
                    waits(d, depth + 1)
                    continue
                if d.eng == 'pe' and e == 'pe' and d.stream is None:
                    continue
                key = id(d.sem)
                if seen.get(key, 0) < d.tick:
                    h.wait_ge(d.sem, d.tick)
                    seen[key] = d.tick

        for op in self.order[e]:
            waits(op)
            if op.fn is None:
                continue
            ins = op.fn(h)
            if op.inc:
                ins.then_inc(op.sem, op.inc)


def build(nl=2, taps=(), stop_after=None):
    nc = bass.Bass("TRN2", target_bir_lowering=False)
    S = Sched(nc)

    def DI(name, shape, dt):
        return nc.dram_tensor(name, list(shape), dt, kind="ExternalInput")

    x_d = DI("x", [S_, D_], F32)
    pT_d = DI("pT", [2, 256, S_], F32)
    pos_d = DI("pos", [128, NT], I32)
    ident_d = DI("ident", [128, 128], BF16)
    J_d = DI("J", [128, 128], F32)
    oh_d = DI("oh", [2, 33, 383], F32)
    invf_d = DI("invf", [128, 16], F32)
    relb_d = DI("rel_bias", [32, 12], F32)
    npre_d = DI("norm_pre", [2, D_], F32)
    npost_d = DI("norm_post", [2, D_], F32)
    win_d = DI("w_in", [2, D_, IN_TOTAL], F32)
    lam_d = DI("da_lambda", [2, 256], F32)
    subln_d = DI("da_subln", [2, 128], F32)
    qn_d = DI("mla_q_norm", [2, 256], F32)
    wqb_d = DI("mla_w_qb", [2, 256, 768], F32)
    kvn_d = DI("mla_kv_norm", [2, 128], F32)
    wkvb_d = DI("mla_w_kvb", [2, 128, 1024], F32)
    sink_d = DI("sw_sinks", [2, 8], F32)
    wbr_d = [DI("w_br_a", [2, 512, D_], F32), DI("w_br_b", [2, 512, D_], F32), DI("w_br_c", [2, 512, D_], F32)]
    wout_d = DI("w_out", [2, D_, D_], F32)
    wpg_d = DI("w_ple_gate", [2, D_, D_], F32)
    wpp_d = DI("w_ple_proj", [2, 256, D_], F32)
    out_d = nc.dram_tensor("out", [S_, D_], F32, kind="ExternalOutput")
    scr_d = nc.dram_tensor("scr", [12, 383], F32, kind="Internal")
    tap_d = {}
    for (nm, shp, dt) in taps:
        tap_d[nm] = nc.dram_tensor("tap_" + nm, list(shp), dt, kind="ExternalOutput")

    ARENA = 212736
    arena = nc.alloc_sbuf_tensor("arena", [128, ARENA], U8)
    top = [0]

    def alloc(shape, dt, nbytes_el):
        n = 1
        for s in shape[1:]:
            n *= s
        nb = n * nbytes_el
        nb = (nb + 63) // 64 * 64
        off = top[0]
        top[0] += nb
        assert top[0] <= ARENA, ("SBUF overflow", top[0])
        a = arena[:, off:off + n * nbytes_el].bitcast(dt)
        if len(shape) == 3:
            a = a.rearrange("p (a b) -> p a b", b=shape[2])
        elif len(shape) == 4:
            a = a.rearrange("p (a b c) -> p a b c", b=shape[2], c=shape[3])
        return a

    def f32(*shape): return alloc([128] + list(shape), F32, 4)
    def b16(*shape): return alloc([128] + list(shape), BF16, 2)
    def i32(*shape): return alloc([128] + list(shape), I32, 4)

    ps_t = nc.alloc_psum_tensor("ps", [128, 8, 512], F32)

    def PS(b, n=512, nb=1):
        if nb == 1:
            return ps_t[:, b, 0:n]
        return ps_t[:, b:b + nb, :]

    def PSB(b):
        return ps_t[:, b, :].bitcast(BF16)

    xres = f32(NT, D_)
    hT = b16(8, S_)
    uT = b16(12, S_)
    UT_BASE = top[0] - 12 * S_ * 2
    ident = b16(128)
    cosT = f32(NT, 16)
    sinT = f32(NT, 16)
    cbias = f32(12)
    maskT = f32(128)
    Jsb = f32(128)
    stat = f32(128)
    stati = i32(16)
    PERSIST_TOP = top[0]

    def dma(q, out, in_, reads=(), writes=(), stream=None):
        assert stream is not None
        return S.add(q, lambda h: h.dma_start(out=out, in_=in_), reads, writes, stream)

    def mm(out, lhsT, rhs, start, stop, reads, writes, skip=False):
        if skip:
            return S.add('pe', lambda h: h.matmul(out, lhsT, rhs, start=start, stop=stop, skip_group_check=True), reads, writes)
        return S.add('pe', lambda h: h.matmul(out, lhsT, rhs, start=start, stop=stop), reads, writes)

    def tr(out, in_, reads, writes):
        np_ = in_.shape[0]
        return S.add('pe', lambda h: h.transpose(out, in_, ident[0:np_, 0:np_]), reads, writes)

    def act(out, in_, func, reads, writes, bias=0.0, scale=1.0, accum=None):
        if accum is None:
            return S.add('act', lambda h: h.activation(out=out, in_=in_, func=func, bias=bias, scale=scale), reads, writes)
        return S.add('act', lambda h: h.activation(out=out, in_=in_, func=func, bias=bias, scale=scale, accum_out=accum), reads, writes)

    def cp(eng, out, in_, reads, writes):
        if eng == 'act':
            return S.add(eng, lambda h: h.activation(out=out, in_=in_, func=AF.Copy), reads, writes)
        return S.add(eng, lambda h: h.tensor_copy(out=out, in_=in_), reads, writes)

    def tt(eng, out, a, b, op, reads, writes):
        return S.add(eng, lambda h: h.tensor_tensor(out=out, in0=a, in1=b, op=op), reads, writes)

    def ts(eng, out, a, s1, s2, op0, op1, reads, writes):
        if s2 is None:
            return S.add(eng, lambda h: h.tensor_scalar(out=out, in0=a, scalar1=s1, scalar2=None, op0=op0), reads, writes)
        return S.add(eng, lambda h: h.tensor_scalar(out=out, in0=a, scalar1=s1, scalar2=s2, op0=op0, op1=op1), reads, writes)

    def stt(out, a, s, b, op0, op1, reads, writes, accum=None):
        if accum is None:
            return S.add('dve', lambda h: h.scalar_tensor_tensor(out=out, in0=a, scalar=s, in1=b, op0=op0, op1=op1), reads, writes)
        return S.add('dve', lambda h: h.scalar_tensor_tensor(out=out, in0=a, scalar=s, in1=b, op0=op0, op1=op1, accum_out=accum), reads, writes)

    def memset(eng, out, val, reads, writes):
        return S.add(eng, lambda h: h.memset(out, val), reads, writes)

    def recip(out, in_, reads, writes):
        return S.add('dve', lambda h: h.reciprocal(out=out, in_=in_), reads, writes)

    def bcast_row(handle, off, n):
        return AP(handle, off, [[0, 128], [1, n]])

    def rsqrt_newton(dst, src, n, scale, eps, rk, scratch_f, scratch_i, eng='dve'):
        v = scratch_f[:, 0:n]; y = dst; c = scratch_f[:, n:2 * n]
        ti = scratch_i[:, 0:n]
        R = rk
        ts('dve', v, src, scale, eps, ALU.mult, ALU.add, R, R)
        ts('dve', ti, v.bitcast(I32), 1, None, ALU.logical_shift_right, None, R, R)
        ts('dve', ti, ti, -1.0, float(0x5f3759df), ALU.mult, ALU.add, R, R)
        y0 = ti.bitcast(F32)
        for it in range(3):
            yin = y0 if it == 0 else y
            tt(eng, c, v, yin, ALU.mult, R, R)
            tt(eng, c, c, yin, ALU.mult, R, R)
            ts(eng, c, c, -0.5, 1.5, ALU.mult, ALU.add, R, R)
            tt(eng, y, yin, c, ALU.mult, R, R)

    wq_cnt = [0]

    def load_w(dst, src_handle, l_off, row_stride, r0, nkc, c0, ncols, reads=(), writes=(), stream=None):
        src = AP(src_handle, l_off + r0 * row_stride + c0, [[row_stride, 128], [128 * row_stride, nkc], [1, ncols]])
        return S.add('pool', lambda h: h.dma_start(out=dst, in_=src), reads, writes, stream)

    dma('sp', ident, ident_d.ap(), writes=['ident'], stream='c_ident')
    dma('sp', cbias, bcast_row(relb_d, 31 * 12, 12), writes=['cbias'], stream='c_cb')
    for tt_ in range(NT):
        dma('sp', xres[:, tt_, :], x_d.ap()[tt_ * 128:(tt_ + 1) * 128, :], writes=[('x', tt_)], stream='xin')

    setup_mark = top[0]
    tab33 = f32(12)
    ohsb = f32(2, 383)
    posi = i32(NT)
    posf = f32(NT)
    invf = f32(16)
    ang = f32(NT, 16)
    w1 = f32(NT, 16)
    w2 = f32(NT, 16)
    tvec = f32(383)
    dma('sp', tab33[0:32, :], relb_d.ap(), writes=['tab33'], stream='c_tab')
    memset('dve', tab33[32:33, :], NEG, [], ['tab33'])
    dma('sp', ohsb[0:33, :, :], oh_d.ap().rearrange("t b r -> b t r"), writes=['ohsb'], stream='c_oh')
    dma('sp', Jsb, J_d.ap(), writes=['Jsb'], stream='c_J')
    dma('sp', posi, pos_d.ap(), writes=['posi'], stream='c_pos')
    dma('sp', invf, invf_d.ap(), writes=['invf'], stream='c_invf')
    mm(PS(0, 383)[0:12, :], tab33[0:33, 0:12], ohsb[0:33, 0, :], True, True, ['tab33', 'ohsb'], [('ps', 0)])
    mm(PS(1, 383)[0:12, :], tab33[0:33, 0:12], ohsb[0:33, 1, :], True, True, ['tab33', 'ohsb'], [('ps', 1)])
    cp('dve', tvec[0:12, :], PS(1, 383)[0:12, :], [('ps', 1)], ['tvec'])
    cp('dve', tvec[0:4, :], PS(0, 383)[0:4, :], [('ps', 0), 'tvec'], ['tvec'])
    dma('sp', scr_d.ap(), tvec[0:12, :], reads=['tvec'], writes=['scr'], stream='c_scr')

    cp('dve', posf, posi, ['posi'], ['posf'])
    tt('dve', ang, posf.unsqueeze(2).broadcast_to([128, NT, 16]), invf.unsqueeze(1).broadcast_to([128, NT, 16]), ALU.mult,
       ['posf', 'invf'], ['ang'])
    MAGIC = 12582912.0
    C1 = 6.28125
    C2 = 2.0 * math.pi - 6.28125
    ts('dve', w1, ang, 1.0 / (2.0 * math.pi), MAGIC, ALU.mult, ALU.add, ['ang'], ['w1'])
    ts('dve', w1, w1, -MAGIC, None, ALU.add, None, ['w1'], ['w1'])
    stt(w2, w1, -C1, ang, ALU.mult, ALU.add, ['w1', 'ang'], ['w2'])
    stt(w2, w1, -C2, w2, ALU.mult, ALU.add, ['w1', 'w2'], ['w2'])
    halfpi = f32(1)
    memset('dve', halfpi, math.pi / 2.0, [], ['halfpi'])
    act(w1, w2, AF.Sin, ['w2'], ['w1'], scale=0.5)
    ts('dve', ang, w2, -1.0, None, ALU.mult, None, ['w2'], ['ang'])
    tt('dve', ang, ang, w2, ALU.max, ['ang', 'w2'], ['ang'])
    act(ang, ang, AF.Sin, ['ang', 'halfpi'], ['ang'], scale=-0.5, bias=halfpi[:, 0:1])
    stt(sinT, w1, 2.0, ang, ALU.mult, ALU.mult, ['w1', 'ang'], ['sinT'])
    tt('dve', w2, w1, w1, ALU.mult, ['w1'], ['w2'])
    ts('dve', cosT, w2, -2.0, 1.0, ALU.mult, ALU.add, ['w2'], ['cosT'])
    S.barrier()
    top[0] = setup_mark

    RK = ['stat']
    ptc = [0]
    sbc = [0]
    tsc = [0]

    def attn_block(qb, qT, kT, p0, Kd, v, vres, dv1, scale, span, bias_ap, cb_ap, band, ocol, PTs, tmpS, qkres):
        j_lo = 0 if band is None else max(0, 4 * qb - band)
        js = list(range(j_lo, 4 * qb + 4))
        info = {}

        def geom(j):
            i_lo = max(j, 4 * qb)
            i_hi = 4 * qb + 3 if band is None else min(4 * qb + 3, j + band)
            n_i = i_hi - i_lo + 1
            off = (i_lo - 4 * qb) * 128
            return i_lo, i_hi, n_i, off, n_i * 128

        def issue_S(j):
            i_lo, i_hi, n_i, off, ncols = geom(j)
            sb = 4 + (sbc[0] % 4); sbc[0] += 1
            info[j] = sb
            mm(PS(sb)[:, off:off + ncols], kT[p0:p0 + Kd, j * 128:(j + 1) * 128],
               qT[p0:p0 + Kd, qb * 512 + off:qb * 512 + off + ncols], True, True, qkres, [('ps', sb)])

        def issue_rest(j):
            i_lo, i_hi, n_i, off, ncols = geom(j)
            sb = info[j]
            k = ptc[0] % len(PTs); ptc[0] += 1
            pt = PTs[k]
            d_hi = min(i_hi, j + span - 1)
            nd = d_hi - i_lo + 1 if d_hi >= i_lo else 0
            if nd > 0:
                c0 = (i_lo - j) * 128
                tk = tsc[0] % len(tmpS); tsc[0] += 1
                tb_ = tmpS[tk]
                stt(tb_[:, 0:nd * 128], PS(sb)[:, off:off + nd * 128], scale, bias_ap[:, c0:c0 + nd * 128],
                    ALU.mult, ALU.add, [('ps', sb), 'biasT'], [('tmpS', tk)])
                act(pt[:, off:off + nd * 128], tb_[:, 0:nd * 128], AF.Exp, [('tmpS', tk)], [('pt', k)])
            if n_i > nd:
                o2 = off + nd * 128
                act(pt[:, o2:off + ncols], PS(sb)[:, o2:off + ncols], AF.Exp, [('ps', sb), 'cbias'], [('pt', k)],
                    scale=scale, bias=(cb_ap if cb_ap is not None else 0.0))
            for i in range(i_lo, i_hi + 1):
                mm(PS(i % 4)[:, ocol:ocol + dv1], pt[:, (i - 4 * qb) * 128:(i - 4 * qb + 1) * 128], v[:, j, :],
                   False, False, [('pt', k), vres, ('Oz', i % 4)], [('O', i % 4)], skip=True)

        SKEW = 3
        for idx in range(min(SKEW, len(js))):
            issue_S(js[idx])
        for idx, j in enumerate(js):
            if idx + SKEW < len(js):
                issue_S(js[idx + SKEW])
            issue_rest(j)

    def proj_fm(dst, dres, w, wres, M, src, sres_fn, nkc, rows=None, eng='dve', bank0=6):
        for tb in range(4):
            pb = bank0 + (tb % 2)
            for kc in range(nkc):
                rhs = src[:, kc, tb * 512:(tb + 1) * 512] if nkc > 1 or len(src.shape) == 3 else src[:, tb * 512:(tb + 1) * 512]
                lw = w[:, kc, :] if len(w.shape) == 3 else w
                mm(PS(pb)[0:M, :], lw, rhs, kc == 0, kc == nkc - 1, [wres, sres_fn(tb)], [('ps', pb)])
            r0, r1 = rows if rows is not None else (0, M)
            cp(eng, dst[r0:r1, tb * 512:(tb + 1) * 512], PS(pb)[r0:r1, :], [('ps', pb)], [dres])

    hTres = lambda tb: ('hT', tb)

    def sz_pair(WZ_, szp_, fTb_):
        for tg in range(4):
            pb = 6 + (tg % 2)
            kk = tg % 2
            for t4 in range(4):
                t = tg * 4 + t4
                for kc in range(8):
                    mm(PS(pb)[:, t4 * 128:(t4 + 1) * 128], hT[:, kc, t * 128:(t + 1) * 128], WZ_[:, kc, :], kc == 0, kc == 7,
                       ['WZ', ('hT', tg)], [('ps', pb)])
            act(fTb_[kk], PS(pb), AF.Tanh, [('ps', pb)], [('hn', kk)], scale=0.5)
            stt(szp_[:, tg * 4:(tg + 1) * 4, :].rearrange("p a b -> p (a b)"), fTb_[kk], 1.0, PS(pb), ALU.add, ALU.mult,
                [('hn', kk), ('ps', pb)], ['szp'])
    allhT = [('hT', i) for i in range(4)]

    for l in range(nl):
        lam_init = 0.8 - 0.6 * math.exp(-0.3 * l)
        mark_layer = top[0]
        hn = [b16(D_), b16(D_)]
        junk = b16(D_)
        mark_s0 = top[0]
        gpre = f32(D_)
        dma('sp', gpre, bcast_row(npre_d, l * D_, D_), writes=['gpre'], stream='g_pre')
        for t in range(NT):
            act(junk, xres[:, t, :], AF.Square, [('x', t), 'junk'], ['junk', 'stat'], accum=stat[:, t:t + 1])
        rsqrt_newton(stat[:, 16:32], stat[:, 0:16], 16, 1.0 / D_, EPS, RK, stat[:, 64:96], stati)
        for t in range(NT):
            b = t % 2
            stt(hn[b], xres[:, t, :], stat[:, 16 + t:17 + t], gpre, ALU.mult, ALU.mult,
                [('x', t), 'stat', 'gpre'], [('hn', b)])
            pb = 6 + b
            for kc in range(8):
                tr(PSB(pb)[:, kc * 128:(kc + 1) * 128], hn[b][:, kc * 128:(kc + 1) * 128], [('hn', b), 'ident'], [('ps', pb)])
            cp('act' if b else 'dve', hT[:, :, t * 128:(t + 1) * 128],
               PSB(pb).rearrange("p (a b) -> p a b", b=128), [('ps', pb)], [('hT', t // 4)])
        S.barrier()
        top[0] = mark_s0
        if 'hT' in tap_d and l == 0:
            dma('sp', tap_d['hT'].ap().rearrange("(kc p) t -> p kc t", p=128), hT, reads=allhT, stream='tap_hT')

        biasT = f32(12, 256)
        mark_mix = top[0]
        Thk = f32(12, 256)
        dma('sp', Thk, AP(scr_d, 0, [[1, 128], [383, 12], [1, 256]]), reads=['scr'], writes=['Thk'], stream='thk')
        for h in range(12):
            pb = 6 + (h % 2)
            mm(PS(pb)[:, 0:256], Jsb, Thk[:, h, :], True, True, ['Jsb', 'Thk'], [('ps', pb)])
            cp('dve', biasT[:, h, :], PS(pb)[:, 0:256], [('ps', pb)], ['biasT'])
        ts('dve', maskT, biasT[:, 0, 0:128], -1000.0, NEG, ALU.is_lt, ALU.mult, ['biasT'], ['biasT'])
        S.barrier()
        top[0] = mark_mix
        if 'biasT' in tap_d and l == 0:
            dma('sp', tap_d['biasT'].ap().rearrange("p (h c) -> p h c", c=256), biasT, reads=['biasT'], stream='tap_bT')
        if stop_after == 'bias':
            break

        PTs = [b16(512), b16(512), b16(512), b16(512)]
        tmpS = [f32(256), f32(256)]
        fTs = [f32(128), f32(128)]
        fZs = None
        fAs = [f32(128), f32(128)]
        rrs = [f32(4), f32(4)]
        fT = fTs[0]
        mark_mix = top[0]

        lamb = f32(256)
        gsub = f32(128)
        WAs = [uT[:, 4 + 2 * i:6 + 2 * i, :].rearrange("p a t -> p (a t)").rearrange("p (k w c) -> p k w c", w=4, c=128)
               for i in range(2)]

        def load_WA(hh):
            for wi, key in enumerate(('a_q', 'a_k', 'a_v', 'a_z')):
                load_w(WAs[hh % 2][:, :, wi, :], win_d, l * D_ * IN_TOTAL, IN_TOTAL, 0, 8, OFF[key] + hh * 128, 128,
                       writes=[('WA', hh % 2)], stream=('WA', hh % 2))
        load_WA(0)
        qT_h = b16(S_)
        kT_h = b16(S_)
        v_h = b16(NT, 129)
        Dm = f32(NT, 128)
        OcpA = [f32(4, 258), f32(4, 258)]
        fT4s = [uT[:, 8, 0:1024].bitcast(F32), uT[:, 8, 1024:2048].bitcast(F32)]
        fZ4s = [uT[:, 9, 0:1024].bitcast(F32), uT[:, 9, 1024:2048].bitcast(F32)]
        dma('sp', lamb, bcast_row(lam_d, l * 256, 256), writes=['lamb'], stream='lamb')
        dma('sp', gsub, bcast_row(subln_d, l * 128, 128), writes=['gsub'], stream='gsub')
        stt(fT[:, 0:64], lamb[:, 0:64], 1.0, lamb[:, 64:128], ALU.mult, ALU.mult, ['lamb'], [('fT', 0), 'stat'], accum=stat[:, 36:37])
        stt(fT[:, 64:128], lamb[:, 128:192], 1.0, lamb[:, 192:256], ALU.mult, ALU.mult, ['lamb'], [('fT', 0), 'stat'], accum=stat[:, 37:38])
        act(stat[:, 38:40], stat[:, 36:38], AF.Exp, ['stat'], ['stat'])
        tt('dve', stat[:, 40:41], stat[:, 39:40], stat[:, 38:39], ALU.subtract, ['stat'], ['stat'])
        ts('dve', stat[:, 40:41], stat[:, 40:41], -lam_init, None, ALU.add, None, ['stat'], ['stat'])
        ts('dve', gsub, gsub, 0.5 * (1.0 - lam_init), None, ALU.mult, None, ['gsub'], ['gsub'])
        memset('dve', v_h[:, :, 128:129], 1.0, [], ['v_h'])
        def projA(hh):
            WA_ = WAs[hh % 2]
            WAr_ = ('WA', hh % 2)
            proj_fm(qT_h, 'qT_h', WA_[:, :, 0, :], WAr_, 128, hT, hTres, 8)
            proj_fm(kT_h, 'kT_h', WA_[:, :, 1, :], WAr_, 128, hT, hTres, 8, eng='act')
            for tg in range(4):
                pb = 6 + (tg % 2)
                for t4 in range(4):
                    t = tg * 4 + t4
                    for kc in range(8):
                        mm(PS(pb)[:, t4 * 128:(t4 + 1) * 128], hT[:, kc, t * 128:(t + 1) * 128], WA_[:, kc, 2, :],
                           kc == 0, kc == 7, [WAr_, ('hT', tg)], [('ps', pb)])
                cp('dve', v_h[:, tg * 4:(tg + 1) * 4, 0:128], PS(pb).rearrange("p (a b) -> p a b", b=128), [('ps', pb)], ['v_h'])
        projA(0)
        for h in range(4):
            WA = WAs[h % 2]
            WAr = ('WA', h % 2)
            if h + 1 < 4:
                load_WA(h + 1)
            allO = [('O', i_) for i_ in range(4)]
            allOz = [('Oz', i_) for i_ in range(4)]
            memset('dve', ps_t[:, 0:4, 0:258], 0.0, [], allO + allOz)
            for qb in range(4):
                for m in range(2):
                    attn_block(qb, qT_h, kT_h, m * 64, 64, v_h, 'v_h', 129, 0.125, 2, biasT[:, h, :], cbias[:, h:h + 1],
                               None, m * 129, PTs, tmpS, ['qT_h', 'kT_h'])
                oc = OcpA[qb % 2]
                ocr = ('OcpA', qb % 2)
                cp('dve', oc, ps_t[:, 0:4, 0:258], allO, [ocr])
                if qb < 3:
                    memset('dve', ps_t[:, 0:4, 0:258], 0.0, [], allO + allOz)
                for i4 in range(4):
                    t = qb * 4 + i4
                    O = oc[:, i4, :]
                    Ov = O.rearrange("p (m c) -> p m c", c=129)
                    kk = i4 % 2
                    rr = rrs[kk]; fA = fAs[kk]
                    recip(rr[:, 0:2], Ov[:, :, 128], [ocr], [('rr', kk)])
                    tt('dve', rr[:, 1:2], rr[:, 1:2], stat[:, 40:41], ALU.mult, [('rr', kk), 'stat'], [('rr', kk)])
                    ts('dve', fA, O[:, 129:257], rr[:, 1:2], None, ALU.mult, None, [ocr, ('rr', kk)], [('fA', kk)])
                    stt(Dm[:, t, :], O[:, 0:128], rr[:, 0:1], fA, ALU.mult, ALU.add, [ocr, ('rr', kk), ('fA', kk)], ['Dm'])
                    stt(junk[:, 0:128], Dm[:, t, :], 1.0, Dm[:, t, :], ALU.mult, ALU.mult, ['Dm', 'junk'], ['junk', 'stat'],
                        accum=stat[:, t:t + 1])
            if h + 1 < 4:
                projA(h + 1)
            rsqrt_newton(stat[:, 16:32], stat[:, 0:16], 16, 1.0 / 128.0, EPS, RK, stat[:, 64:96], stati)
            for tg in range(4):
                pb = 6 + (tg % 2)
                kk = tg % 2
                for t4 in range(4):
                    t = tg * 4 + t4
                    for kc in range(8):
                        mm(PS(pb)[:, t4 * 128:(t4 + 1) * 128], hT[:, kc, t * 128:(t + 1) * 128], WA[:, kc, 3, :], kc == 0, kc == 7,
                           [WAr, ('hT', tg)], [('ps', pb)])
                fT4 = fT4s[kk]; fZ4 = fZ4s[kk]
                act(fT4, PS(pb), AF.Tanh, [('ps', pb)], [('fT4', kk)], scale=0.5)
                stt(fZ4, fT4, 1.0, PS(pb), ALU.add, ALU.mult, [('fT4', kk), ('ps', pb)], [('fZ4', kk)])
                D4 = Dm[:, tg * 4:(tg + 1) * 4, :]
                f3 = fT4.rearrange("p (a b) -> p a b", b=128)
                tt('dve', f3, D4, stat[:, 16 + tg * 4:20 + tg * 4].unsqueeze(2).broadcast_to([128, 4, 128]), ALU.mult,
                   ['Dm', 'stat', ('fZ4', kk)], [('fT4', kk)])
                tt('dve', f3, f3, gsub.unsqueeze(1).broadcast_to([128, 4, 128]), ALU.mult, [('fT4', kk), 'gsub'], [('fT4', kk)])
                ub = hn[kk]
                tt('dve', ub[:, 0:512], fT4, fZ4, ALU.mult, [('fT4', kk), ('fZ4', kk)], [('hn', kk)])
                for t4 in range(4):
                    tr(PSB(pb)[:, t4 * 128:(t4 + 1) * 128], ub[:, t4 * 128:(t4 + 1) * 128], [('hn', kk), 'ident'], [('ps', pb)])
                cp('act', uT[:, h, tg * 512:(tg + 1) * 512], PSB(pb)[:, 0:512], [('ps', pb)], ['uT'])
        S.barrier()
        top[0] = mark_mix
        if 'uA' in tap_d and l == 0:
            dma('sp', tap_d['uA'].ap().rearrange("(kc p) t -> p kc t", p=128), uT[:, 0:4, :], reads=['uT'], stream='tap_uA')
        if stop_after == 'A':
            break

        gq = f32(256)
        gkv = f32(128)
        wqb = b16(2, 768)
        wkvb = b16(1024)
        qlatT = uT[:, 8:10, :]
        kvlatT = uT[:, 10, :]
        kropeT = uT[:, 11, :]
        WZ = b16(8, 128)
        qT_h = b16(S_)
        kT_h = b16(S_)
        vm_h = b16(NT, 65)
        q_tm = b16(4, 96)
        u_pair = b16(NT, 128)
        fTb = [hn[0].bitcast(F32), hn[1].bitcast(F32)]
        krt = b16(96)
        r6 = [f32(4, 16) for _ in range(6)]
        mark_b = top[0]
        Wlat = b16(8, 416)
        dma('sp', gq, bcast_row(qn_d, l * 256, 256), writes=['gq'], stream='gq')
        dma('sp', gkv, bcast_row(kvn_d, l * 128, 128), writes=['gkv'], stream='gkv')
        load_w(Wlat, win_d, l * D_ * IN_TOTAL, IN_TOTAL, 0, 8, OFF['b_lat'], 416, writes=['Wlat'], stream='Wlat')
        load_w(wqb, wqb_d, l * 256 * 768, 768, 0, 2, 0, 768, writes=['wqb'], stream='wqb')
        S.add('pool', lambda h_, o=wkvb, i=AP(wkvb_d, l * 128 * 1024, [[1024, 128], [1, 1024]]): h_.dma_start(out=o, in_=i),
              (), ['wkvb'], 'wkvb')
        memset('dve', krt[:, 0:64], 0.0, [], ['krt'])
        memset('dve', vm_h[:, :, 64:65], 2.0, [], ['vm_h'])
        for t in range(NT):
            pb = 6 + (t % 2)
            for kc in range(8):
                mm(PS(pb)[:, 0:416], hT[:, kc, t * 128:(t + 1) * 128], Wlat[:, kc, :], kc == 0, kc == 7,
                   ['Wlat', ('hT', t // 4)], [('ps', pb)])
            act(junk[:, 0:256], PS(pb)[:, 0:256], AF.Square, [('ps', pb), 'junk'], ['junk', 'stat'], accum=stat[:, t:t + 1])
            act(junk[:, 256:384], PS(pb)[:, 256:384], AF.Square, [('ps', pb), 'junk'], ['junk', 'stat'], accum=stat[:, 42 + t:43 + t])
        rsqrt_newton(stat[:, 16:32], stat[:, 0:16], 16, 1.0 / 256.0, EPS, RK, stat[:, 64:96], stati)
        rkv = rr
        rkv = f32(16)
        rsqrt_newton(rkv, stat[:, 42:58], 16, 1.0 / 128.0, EPS, RK + ['rkv'], f32(32), stati)
        for t in range(NT):
            pb = 6 + (t % 2)
            b = t % 2
            for kc in range(8):
                mm(PS(pb)[:, 0:416], hT[:, kc, t * 128:(t + 1) * 128], Wlat[:, kc, :], kc == 0, kc == 7,
                   ['Wlat', ('hT', t // 4)], [('ps', pb)])
            stt(hn[b][:, 0:256], PS(pb)[:, 0:256], stat[:, 16 + t:17 + t], gq, ALU.mult, ALU.mult,
                [('ps', pb), 'stat', 'gq'], [('hn', b)])
            stt(hn[b][:, 256:384], PS(pb)[:, 256:384], rkv[:, t:t + 1], gkv, ALU.mult, ALU.mult,
                [('ps', pb), 'stat', 'rkv', 'gkv'], [('hn', b)])
            x1 = PS(pb)[:, 384:400]; x2 = PS(pb)[:, 400:416]
            cs = cosT[:, t, :]; sn = sinT[:, t, :]
            a_, b_, c_, d_ = r6[0][:, 0, :], r6[1][:, 0, :], r6[2][:, 0, :], r6[3][:, 0, :]
            tt('dve', a_, x1, cs, ALU.mult, [('ps', pb), 'cosT'], ['r6'])
            tt('dve', b_, x2, sn, ALU.mult, [('ps', pb), 'sinT'], ['r6'])
            tt('dve', c_, x2, cs, ALU.mult, [('ps', pb), 'cosT'], ['r6'])
            tt('dve', d_, x1, sn, ALU.mult, [('ps', pb), 'sinT'], ['r6'])
            tt('dve', krt[:, 64:80], a_, b_, ALU.subtract, ['r6'], ['krt'])
            tt('dve', krt[:, 80:96], c_, d_, ALU.add, ['r6'], ['krt'])
            pt_ = 4 + (t % 2)
            tr(PSB(pt_)[:, 0:128], hn[b][:, 0:128], [('hn', b), 'ident'], [('ps', pt_)])
            tr(PSB(pt_)[:, 128:256], hn[b][:, 128:256], [('hn', b), 'ident'], [('ps', pt_)])
            tr(PSB(pt_)[:, 256:384], hn[b][:, 256:384], [('hn', b), 'ident'], [('ps', pt_)])
            tr(PSB(pt_)[0:96, 384:512], krt[:, 0:96], ['krt', 'ident'], [('ps', pt_)])
            cp('act', qlatT[:, :, t * 128:(t + 1) * 128], PSB(pt_)[:, 0:256].rearrange("p (a b) -> p a b", b=128),
               [('ps', pt_)], ['qlatT'])
            cp('act', kvlatT[:, t * 128:(t + 1) * 128], PSB(pt_)[:, 256:384], [('ps', pt_)], ['kvlatT'])
            cp('act', kropeT[64:96, t * 128:(t + 1) * 128], PSB(pt_)[64:96, 384:512], [('ps', pt_)], ['kropeT'])
        S.barrier()
        top[0] = mark_b
        szp = uT[:, 7, :].rearrange("p (a b) -> p a b", b=128)
        Ocp = [f32(4, 65), f32(4, 65)]
        if stop_after == 'B0':
            break
        for h in range(8):
            if stop_after == 'B1' and h == 1:
                break
            if h % 2 == 0:
                load_w(WZ, win_d, l * D_ * IN_TOTAL, IN_TOTAL, 0, 8, OFF['b_z'] + h * 64, 128, writes=['WZ'], stream='WZ')
                sz_pair(WZ, szp, fTb)
            for tb in range(4):
                pb = 6 + (tb % 2)
                for t4 in range(4):
                    t = tb * 4 + t4
                    for kc in range(2):
                        mm(PS(pb)[:, t4 * 96:(t4 + 1) * 96], qlatT[:, kc, t * 128:(t + 1) * 128], wqb[:, kc, h * 96:(h + 1) * 96],
                           kc == 0, kc == 1, ['qlatT', 'wqb'], [('ps', pb)])
                import os as _os
                QS = int(_os.environ.get("QSTEP", "9"))
                P3 = PS(pb)[:, 0:384].rearrange("p (a b) -> p a b", b=96)
                x1 = P3[:, :, 64:80]; x2 = P3[:, :, 80:96]
                cs = cosT[:, tb * 4:(tb + 1) * 4, :]; sn = sinT[:, tb * 4:(tb + 1) * 4, :]
                if QS >= 1:
                    tt('dve', r6[0], x1, cs, ALU.mult, [('ps', pb), 'cosT'], ['r6'])
                    tt('dve', r6[1], x2, sn, ALU.mult, [('ps', pb), 'sinT'], ['r6'])
                    tt('dve', r6[2], x2, cs, ALU.mult, [('ps', pb), 'cosT'], ['r6'])
                    tt('dve', r6[3], x1, sn, ALU.mult, [('ps', pb), 'sinT'], ['r6'])
                if QS >= 2:
                    tt('dve', q_tm[:, :, 64:80], r6[0], r6[1], ALU.subtract, ['r6'], ['q_tm'])
                    tt('dve', q_tm[:, :, 80:96], r6[2], r6[3], ALU.add, ['r6'], ['q_tm'])
                if QS >= 3:
                    cp('dve', q_tm[:, :, 0:64], P3[:, :, 0:64], [('ps', pb)], ['q_tm'])
                pt_ = 4 + (tb % 2)
                if QS >= 4:
                    for t4 in range(4):
                        tr(PSB(pt_)[0:96, t4 * 128:(t4 + 1) * 128], q_tm[:, t4, :], ['q_tm', 'ident'], [('ps', pt_)])
                if QS >= 5:
                    cp('dve', qT_h[0:96, tb * 512:(tb + 1) * 512], PSB(pt_)[0:96, 0:512], [('ps', pt_)], ['qT_h'])
                if QS < 9:
                    break
            if stop_after == 'B1a':
                break
            proj_fm(kT_h, 'kT_h', wkvb[:, h * 128:h * 128 + 64], 'wkvb', 64, kvlatT, lambda tb: 'kvlatT', 1, eng='act')
            cp('pool', kT_h[64:96, :], kropeT[64:96, :], ['kropeT'], ['kT_h'])
            if stop_after == 'B1b':
                break
            for tg in range(2):
                pb = 6 + (tg % 2)
                for t8 in range(8):
                    t = tg * 8 + t8
                    mm(PS(pb)[:, t8 * 64:(t8 + 1) * 64], kvlatT[:, t * 128:(t + 1) * 128], wkvb[:, h * 128 + 64:h * 128 + 128],
                       True, True, ['kvlatT', 'wkvb'], [('ps', pb)])
                cp('dve', vm_h[:, tg * 8:(tg + 1) * 8, 0:64], PS(pb).rearrange("p (a b) -> p a b", b=64), [('ps', pb)], ['vm_h'])
            allO = [('O', i_) for i_ in range(4)]
            allOz = [('Oz', i_) for i_ in range(4)]
            zc = (h % 2) * 64
            memset('dve', ps_t[:, 0:4, 0:65], 0.0, [], allO + allOz)
            for qb in range(4):
                attn_block(qb, qT_h, kT_h, 0, 96, vm_h, 'vm_h', 65, 96.0 ** -0.5, 1, maskT, None, None, 0, PTs, tmpS,
                           ['qT_h', 'kT_h'])
                oc = Ocp[qb % 2]
                ocr = ('Ocp', qb % 2)
                cp('dve', oc, ps_t[:, 0:4, 0:65], allO, [ocr])
                if qb < 3:
                    memset('dve', ps_t[:, 0:4, 0:65], 0.0, [], allO + allOz)
                rr = rrs[qb % 2]
                recip(rr[:, 0:4], oc[:, :, 64], [ocr], [('rr', qb % 2)])
                for i4 in range(4):
                    t = qb * 4 + i4
                    stt(u_pair[:, t, zc:zc + 64], oc[:, i4, 0:64], rr[:, i4:i4 + 1], szp[:, t, zc:zc + 64], ALU.mult, ALU.mult,
                        [ocr, ('rr', qb % 2), 'szp'], ['u_pair'])
            if h % 2 == 1:
                for t in range(NT):
                    pb = 6 + (t % 2)
                    tr(PSB(pb)[:, 512:640], u_pair[:, t, :], ['u_pair', 'ident'], [('ps', pb)])
                    cp('act', uT[:, 4 + h // 2, t * 128:(t + 1) * 128], PSB(pb)[:, 512:640], [('ps', pb)], ['uT'])
        if stop_after in ('B1', 'B1a', 'B1b', 'B1c'):
            break
        S.barrier()
        top[0] = mark_mix
        if 'uB' in tap_d and l == 0:
            dma('sp', tap_d['uB'].ap().rearrange("(kc p) t -> p kc t", p=128), uT[:, 4:8, :], reads=['uT'], stream='tap_uB')
        if stop_after == 'B':
            break

        WQ = b16(8, 4, 128)
        WKV = b16(8, 256)
        WZ = b16(8, 128)
        ckT = b16(S_)
        cv = b16(NT, 2, 65)
        qT_h = b16(S_)
        u_pair = b16(NT, 128)
        szp = uT[:, 11, :].rearrange("p (a b) -> p a b", b=128)
        Ocp = [f32(4, 65), f32(4, 65)]
        fTb = [hn[0].bitcast(F32), hn[1].bitcast(F32)]
        sk = f32(8)
        dma('sp', sk, bcast_row(sink_d, l * 8, 8), writes=['sk'], stream='sk')
        act(sk, sk, AF.Exp, ['sk'], ['sk'])
        ts('dve', sk, sk, 2.0, None, ALU.mult, None, ['sk'], ['sk'])
        for kvh in range(2):
            for g in range(4):
                load_w(WQ[:, :, g, kvh * 64:(kvh + 1) * 64], win_d, l * D_ * IN_TOTAL, IN_TOTAL, 0, 8,
                       OFF['c_q'] + kvh * 256 + g * 64, 64, writes=['WQ'], stream='WQ')
        load_w(WKV, win_d, l * D_ * IN_TOTAL, IN_TOTAL, 0, 8, OFF['c_k'], 256, writes=['WKV'], stream='WKV')
        memset('dve', cv[:, :, :, 64:65], 2.0, [], ['cv'])
        proj_fm(ckT, 'ckT', WKV[:, :, 0:128], 'WKV', 128, hT, hTres, 8)
        for tg in range(4):
            pb = 6 + (tg % 2)
            for t4 in range(4):
                t = tg * 4 + t4
                for kc in range(8):
                    mm(PS(pb)[:, t4 * 128:(t4 + 1) * 128], hT[:, kc, t * 128:(t + 1) * 128], WKV[:, kc, 128:256],
                       kc == 0, kc == 7, ['WKV', ('hT', tg)], [('ps', pb)])
            cp('dve', cv[:, tg * 4:(tg + 1) * 4, :, 0:64], PS(pb).rearrange("p (a b c) -> p a b c", b=2, c=64), [('ps', pb)], ['cv'])
        for hq in range(8):
            kvh, g = hq // 4, hq % 4
            if hq % 2 == 0:
                load_w(WZ, win_d, l * D_ * IN_TOTAL, IN_TOTAL, 0, 8, OFF['c_z'] + hq * 64, 128, writes=['WZ'], stream='WZ')
                sz_pair(WZ, szp, fTb)
            proj_fm(qT_h, 'qT_h', WQ[:, :, g, :], 'WQ', 128, hT, hTres, 8, rows=(kvh * 64, kvh * 64 + 64))
            allO = [('O', i_) for i_ in range(4)]
            allOz = [('Oz', i_) for i_ in range(4)]
            zc = (hq % 2) * 64
            memset('dve', ps_t[:, 0:4, 0:65], 0.0, [], allO + allOz)
            for qb in range(4):
                attn_block(qb, qT_h, ckT, kvh * 64, 64, cv[:, :, kvh, :], 'cv', 65, 0.125, 2, biasT[:, 4 + hq, :], None, 1, 0,
                           PTs, tmpS, ['qT_h', 'ckT'])
                oc = Ocp[qb % 2]
                ocr = ('Ocp', qb % 2)
                cp('dve', oc, ps_t[:, 0:4, 0:65], allO, [ocr])
                if qb < 3:
                    memset('dve', ps_t[:, 0:4, 0:65], 0.0, [], allO + allOz)
                rr = rrs[qb % 2]
                ts('dve', rr[:, 0:4], oc[:, :, 64], sk[:, hq:hq + 1], None, ALU.add, None, [ocr, 'sk'], [('rr', qb % 2)])
                recip(rr[:, 0:4], rr[:, 0:4], [('rr', qb % 2)], [('rr', qb % 2)])
                for i4 in range(4):
                    t = qb * 4 + i4
                    stt(u_pair[:, t, zc:zc + 64], oc[:, i4, 0:64], rr[:, i4:i4 + 1], szp[:, t, zc:zc + 64], ALU.mult, ALU.mult,
                        [ocr, ('rr', qb % 2), 'szp'], ['u_pair'])
            if hq % 2 == 1:
                for t in range(NT):
                    pb = 6 + (t % 2)
                    tr(PSB(pb)[:, 512:640], u_pair[:, t, :], ['u_pair', 'ident'], [('ps', pb)])
                    cp('act', uT[:, 8 + hq // 2, t * 128:(t + 1) * 128], PSB(pb)[:, 512:640], [('ps', pb)], ['uT'])
        S.barrier()
        top[0] = mark_s0
        if 'uC' in tap_d and l == 0:
            dma('sp', tap_d['uC'].ap().rearrange("(kc p) t -> p kc t", p=128), uT[:, 8:12, :], reads=['uT'], stream='tap_uC')
        if stop_after == 'C':
            break

        y2buf = b16(7, S_)
        mark_m2 = top[0]
        WG = [b16(8, 3, 128), b16(8, 3, 128)]
        WB = [b16(4, 3, 128), b16(4, 3, 128)]
        Tsb = [hn[0].bitcast(F32), hn[1].bitcast(F32)]
        acc = f32(512)
        tmpm = junk.bitcast(F32)
        def load_M1(cc):
            wb_ = cc % 2
            for br in range(3):
                load_w(WG[wb_][:, :, br, :], win_d, l * D_ * IN_TOTAL, IN_TOTAL, 0, 8, OFF['g'] + br * 1024 + cc * 128, 128,
                       writes=[('WG', wb_)], stream=('WG', wb_))
                load_w(WB[wb_][:, :, br, :], wbr_d[br], l * 512 * D_, D_, 0, 4, cc * 128, 128,
                       writes=[('WB', wb_)], stream=('WB', wb_))
        load_M1(0)
        for c in range(8):
            wb = c % 2
            if c + 1 < 8:
                load_M1(c + 1)
            for tb in range(4):
                for br in range(3):
                    pg = 2 * ((tb * 3 + br) % 4)
                    pbk = pg + 1
                    for kc in range(8):
                        mm(PS(pg), WG[wb][:, kc, br, :], hT[:, kc, tb * 512:(tb + 1) * 512], kc == 0, kc == 7,
                           [('WG', wb), ('hT', tb)], [('ps', pg)])
                    for kc in range(4):
                        mm(PS(pbk), WB[wb][:, kc, br, :], uT[:, br * 4 + kc, tb * 512:(tb + 1) * 512], kc == 0, kc == 3,
                           [('WB', wb), 'uT'], [('ps', pbk)])
                    k = (tb * 3 + br) % 2
                    act(Tsb[k], PS(pg), AF.Tanh, [('ps', pg)], [('Tsb', k)], scale=0.5)
                    if br == 0:
                        stt(acc, Tsb[k], 1.0, PS(pbk), ALU.add, ALU.mult, [('Tsb', k), ('ps', pbk)], ['acc'])
                    elif br == 1:
                        stt(tmpm, Tsb[k], 1.0, PS(pbk), ALU.add, ALU.mult, [('Tsb', k), ('ps', pbk)], ['tmpm'])
                        tt('dve', acc, acc, tmpm, ALU.add, ['acc', 'tmpm'], ['acc'])
                    else:
                        stt(tmpm, Tsb[k], 1.0, PS(pbk), ALU.add, ALU.mult, [('Tsb', k), ('ps', pbk)], ['tmpm'])
                        if c < 7:
                            tt('dve', y2buf[:, c, tb * 512:(tb + 1) * 512], acc, tmpm, ALU.add, ['acc', 'tmpm'], ['y2'])
                        else:
                            tt('dve', hT[:, 0, tb * 512:(tb + 1) * 512], acc, tmpm, ALU.add, ['acc', 'tmpm'], [('hT', tb), 'y2'])
        S.barrier()
        top[0] = mark_m2
        if 'y2' in tap_d and l == 0:
            dma('sp', tap_d['y2'].ap()[0:896, :].rearrange("(kc p) t -> p kc t", p=128), y2buf, reads=['y2'], stream='tap_y2')
            dma('sp', tap_d['y2'].ap()[896:1024, :], hT[:, 0, :], reads=['y2'], stream='tap_y2')
        if stop_after == 'M1':
            break

        def ualloc(nbytes):
            o = ualloc.off; ualloc.off += nbytes
            assert ualloc.off <= UT_BASE + 12 * S_ * 2
            return o
        ualloc.off = UT_BASE

        def uview(shape, dt, el):
            n = 1
            for s_ in shape:
                n *= s_
            o = ualloc(n * el)
            a = arena[:, o:o + n * el].bitcast(dt)
            if len(shape) == 2:
                a = a.rearrange("p (a b) -> p a b", b=shape[1])
            return a
        wout = uview([8, D_], BF16, 2)
        wpg = uview([8, D_], BF16, 2)
        wpp = uview([2, D_], BF16, 2)
        pTl = uview([2, S_], BF16, 2)
        gpost = uview([D_], F32, 4)
        Tg = f32(D_)
        t1 = f32(D_)
        x1b = b16(D_)
        x1T = b16(8, 128)
        for half in range(2):
            load_w(wout[:, :, half * 512:(half + 1) * 512], wout_d, l * D_ * D_, D_, 0, 8, half * 512, 512, writes=['wout'], stream='wout')
        for half in range(2):
            load_w(wpg[:, :, half * 512:(half + 1) * 512], wpg_d, l * D_ * D_, D_, 0, 8, half * 512, 512, writes=['wpg'], stream='wpg')
        load_w(wpp[:, :, 0:512], wpp_d, l * 256 * D_, D_, 0, 2, 0, 512, writes=['wpp'], stream='wpp')
        load_w(wpp[:, :, 512:1024], wpp_d, l * 256 * D_, D_, 0, 2, 512, 512, writes=['wpp'], stream='wpp')
        for half in range(2):
            S.add('pool', lambda h_, o=pTl[:, :, half * 1024:(half + 1) * 1024],
                  i=AP(pT_d, l * 256 * S_ + half * 1024, [[S_, 128], [128 * S_, 2], [1, 1024]]): h_.dma_start(out=o, in_=i),
                  (), ['pTl'], 'pTl')
        dma('sp', gpost, bcast_row(npost_d, l * D_, D_), writes=['gpost'], stream='gpost')
        Vgs = [t1, Tg, f32(D_)]
        x1T7 = b16(S_)

        for t in range(NT):
            r = t % 3
            vb = 2 * r
            V2 = ps_t[:, vb:vb + 2, :].rearrange("p a b -> p (a b)")
            for half in range(2):
                for kc in range(8):
                    lhs = y2buf[:, kc, t * 128:(t + 1) * 128] if kc < 7 else hT[:, 0, t * 128:(t + 1) * 128]
                    mm(PS(vb + half), lhs, wout[:, kc, half * 512:(half + 1) * 512], kc == 0, kc == 7, ['y2', 'wout'],
                       [('ps', vb + half)])
            R = [('st2', r)]
            act(junk, V2, AF.Square, [('ps', vb), ('ps', vb + 1), 'junk'], ['junk'] + R, accum=stat[:, t:t + 1])
            stt(Vgs[r], V2, 1.0, gpost, ALU.mult, ALU.mult, [('ps', vb), ('ps', vb + 1), 'gpost'] + R, [('t1', r)])
            rsqrt_newton(stat[:, 16 + t:17 + t], stat[:, t:t + 1], 1, 1.0 / D_, 4.0 * EPS, R, stat[:, 64 + 2 * r:66 + 2 * r],
                         stati[:, r:r + 1], eng='pool')
            stt(xres[:, t, :], Vgs[r], stat[:, 16 + t:17 + t], xres[:, t, :], ALU.mult, ALU.add, [('t1', r), ('x', t)] + R, [('x', t)])
            p = t % 2
            cp('act', hn[p], xres[:, t, :], [('x', t)], [('hn', p)])
            pb = 6 + p
            for kc in range(8):
                tr(PSB(pb)[:, kc * 128:(kc + 1) * 128], hn[p][:, kc * 128:(kc + 1) * 128], [('hn', p), 'ident'], [('ps', pb)])
            Pv = PSB(pb).rearrange("p (a b) -> p a b", b=128)
            cp('dve', hT[:, 1:8, t * 128:(t + 1) * 128], Pv[:, 0:7, :], [('ps', pb)], ['x1T'])
            cp('dve', x1T7[:, t * 128:(t + 1) * 128], Pv[:, 7, :], [('ps', pb)], ['x1T'])

        for t in range(NT):
            p = t % 2
            gb = 4 * p
            G2 = ps_t[:, gb:gb + 2, :].rearrange("p a b -> p (a b)")
            P2 = ps_t[:, gb + 2:gb + 4, :].rearrange("p a b -> p (a b)")
            for half in range(2):
                for kc in range(8):
                    lhs = hT[:, 1 + kc, t * 128:(t + 1) * 128] if kc < 7 else x1T7[:, t * 128:(t + 1) * 128]
                    mm(PS(gb + half), lhs, wpg[:, kc, half * 512:(half + 1) * 512], kc == 0, kc == 7, ['x1T', 'wpg'],
                       [('ps', gb + half)])
                for kc in range(2):
                    mm(PS(gb + 2 + half), pTl[:, kc, t * 128:(t + 1) * 128], wpp[:, kc, half * 512:(half + 1) * 512], kc == 0, kc == 1,
                       ['pTl', 'wpp'], [('ps', gb + 2 + half)])
            act(Vgs[p], G2, AF.Tanh, [('ps', gb), ('ps', gb + 1)], [('t1', p)], scale=0.5)
            stt(Vgs[p], Vgs[p], 1.0, P2, ALU.add, ALU.mult, [('t1', p), ('ps', gb + 2), ('ps', gb + 3)], [('t1', p)])
            stt(xres[:, t, :], Vgs[p], 0.5, xres[:, t, :], ALU.mult, ALU.add, [('t1', p), ('x', t)], [('x', t)])
            if l == nl - 1:
                dma('sp', out_d.ap()[t * 128:(t + 1) * 128, :], xres[:, t, :], reads=[('x', t)], stream='out')
        S.barrier()
        top[0] = mark_layer

    fin_ops = []
    for k, v in S.streams.items():
        if str(k).startswith('out') or str(k).startswith('tap'):
            fin_ops.append(v[-1])
    op = S.add('sp', None)
    op.deps = set(fin_ops)

    sems = []

    def sem_alloc(name):
        s = nc.alloc_semaphore(name.replace("(", "_").replace(")", "_").replace(",", "_").replace("'", "").replace(" ", ""))
        sems.append(s)
        return s

    S.finalize(sem_alloc)
    with nc.Block() as block:
        @block.tensor
        def _(h): S.emit('pe', h)

        @block.scalar
        def _(h): S.emit('act', h)

        @block.vector
        def _(h): S.emit('dve', h)

        @block.gpsimd
        def _(h): S.emit('pool', h)

        @block.sync
        def _(h): S.emit('sp', h)
    return nc


def _t5_bucket_np(n):
    n = np.maximum(n, 0)
    nf = np.maximum(n, 1).astype(np.float32)
    large = 16 + (np.log(nf / np.float32(16)) / np.float32(math.log(8.0)) * np.float32(16)).astype(np.int32)
    large = np.minimum(large, 31)
    return np.where(n < 16, n, large)


def host_consts():
    ident = np.eye(128, dtype=np.float32).astype(ml_dtypes.bfloat16)
    J = np.ascontiguousarray(np.eye(128, dtype=np.float32)[::-1])
    oh = np.zeros((2, 33, 383), np.float32)
    for r in range(383):
        rel = r - 127
        if rel < 0:
            oh[0, 32, r] = 1.0
            oh[1, 32, r] = 1.0
        else:
            b = int(_t5_bucket_np(np.array([rel]))[0])
            oh[0, b, r] = 1.0
            if rel < 128:
                oh[1, b, r] = 1.0
            else:
                oh[1, 32, r] = 1.0
    half = 16
    invf = np.power(np.float32(10000.0), -(np.arange(half, dtype=np.float32) / np.float32(half))).astype(np.float32)
    invf = np.ascontiguousarray(np.broadcast_to(invf[None, :], (128, 16)))
    return dict(ident=ident, J=J, oh=oh, invf=invf)


def prep_inputs(inputs, cores=range(8)):
    c = host_consts()
    f = lambda a: np.ascontiguousarray(np.asarray(a))
    shared = dict(c)
    shared["rel_bias"] = f(inputs["rel_bias"])
    for k in ("norm_pre", "norm_post", "w_in", "da_subln", "mla_q_norm", "mla_w_qb", "mla_kv_norm",
              "mla_w_kvb", "sw_sinks", "w_br_a", "w_br_b", "w_br_c", "w_out", "w_ple_gate", "w_ple_proj"):
        shared[k] = f(inputs[k])
    shared["da_lambda"] = f(np.asarray(inputs["da_lambda"]).reshape(2, 256))
    x = np.asarray(inputs["x"]); p = np.asarray(inputs["p"]); pos = np.asarray(inputs["positions"])
    maps = []
    for b in cores:
        m = dict(shared)
        m["x"] = f(x[b])
        m["pT"] = f(np.transpose(p[:, b], (0, 2, 1)))
        m["pos"] = f(pos[b].reshape(16, 128).T.astype(np.int32))
        maps.append(m)
    return maps


_NC_CACHE = {}


def kernel(**inputs):
    if "nc" not in _NC_CACHE:
        _NC_CACHE["nc"] = build(nl=2)
    nc = _NC_CACHE["nc"]
    maps = prep_inputs(inputs)
    res = run_bass_kernel_spmd(nc, maps, core_ids=list(range(8)))
    out = np.stack([np.asarray(r["out"]) for r in res.results], axis=0)
    return out.astype(np.float32)
```

```python
import math
import numpy as np
import ml_dtypes
import concourse.bass as bass
import concourse.mybir as mybir
from concourse.ap import AP
from concourse.bass_utils import run_bass_kernel_spmd

F32 = mybir.dt.float32
BF16 = mybir.dt.bfloat16
I32 = mybir.dt.int32
U8 = mybir.dt.uint8
AF = mybir.ActivationFunctionType
ALU = mybir.AluOpType

S_ = 2048
D_ = 1024
NT = 16
EPS = 1e-6
NEG = -30000.0
IN_TOTAL = 7328
OFF = dict(a_q=0, a_k=512, a_v=1024, a_z=1536, b_lat=2048, b_z=2464,
           c_q=2976, c_k=3488, c_v=3616, c_z=3744, g=4256)
ENGS = ('pe', 'act', 'dve', 'pool', 'sp')


class Op:
    __slots__ = ('eng', 'fn', 'deps', 'sem', 'tick', 'inc', 'stream', 'needed', 'idx')


class Sched:
    def __init__(self, nc):
        self.nc = nc
        self.order = {e: [] for e in ENGS}
        self.res = {}
        self.streams = {}
        self.n = 0

    def add(self, eng, fn, reads=(), writes=(), stream=None):
        op = Op()
        op.eng = eng; op.fn = fn; op.stream = stream; op.needed = False
        op.sem = None; op.tick = 0; op.inc = 0; op.idx = self.n; self.n += 1
        deps = set()
        for r in reads:
            st = self.res.setdefault(r, [[], [], []])
            deps.update(st[0])
            st[1].append(op)
        for w in writes:
            st = self.res.setdefault(w, [[], [], []])
            if st[1]:
                st[2] = st[1] + st[0]
                st[0] = []; st[1] = []
            deps.update(st[2])
            st[0].append(op)
        deps.discard(op)
        fixed = set()
        for d in deps:
            if d.stream is not None:
                d = self.streams[d.stream][-1]
            fixed.add(d)
        op.deps = fixed
        if stream is not None:
            self.streams.setdefault(stream, []).append(op)
        self.order[eng].append(op)
        return op

    def barrier(self):
        lasts = [self.order[e][-1] for e in ENGS if self.order[e]]
        lasts = [self.streams[d.stream][-1] if d.stream is not None else d for d in lasts]
        for s in self.streams.values():
            lasts.append(s[-1])
        for e in ENGS:
            op = self.add(e, None)
            op.deps = set(x for x in lasts)

    def finalize(self, sem_alloc):
        for e in ENGS:
            for op in self.order[e]:
                for d in op.deps:
                    d.needed = True
        self.eng_sem = {}
        scnt = {}
        self.stream_sem = {}
        for e in ENGS:
            cnt = 0
            sem = None
            for op in self.order[e]:
                if op.stream is not None:
                    if op.stream not in self.stream_sem:
                        self.stream_sem[op.stream] = sem_alloc('st_%s' % str(op.stream))
                    c = scnt.get(op.stream, 0) + 16
                    scnt[op.stream] = c
                    op.sem = self.stream_sem[op.stream]; op.tick = c; op.inc = 16
                elif op.needed and op.fn is not None:
                    if sem is None or cnt >= 30000:
                        sem = sem_alloc('eng_%s_%d' % (e, op.idx)); cnt = 0
                    cnt += 1
                    op.sem = sem; op.tick = cnt; op.inc = 1
                elif op.needed:
                    pass

    def emit(self, e, h):
        seen = {}

        def waits(op, depth=0):
            for d in sorted(op.deps, key=lambda o: o.idx):
                if d.fn is None and d.stream is None:
                    waits(d, depth + 1)
                    continue
                if d.eng == 'pe' and e == 'pe' and d.stream is None:
                    continue
                key = id(d.sem)
                if seen.get(key, 0) < d.tick:
                    h.wait_ge(d.sem, d.tick)
                    seen[key] = d.tick

        for op in self.order[e]:
            waits(op)
            if op.fn is None:
                continue
            ins = op.fn(h)
            if op.inc:
                ins.then_inc(op.sem, op.inc)


def build(nl=2, taps=(), stop_after=None):
    nc = bass.Bass("TRN2", target_bir_lowering=False)
    S = Sched(nc)

    def DI(name, shape, dt):
        return nc.dram_tensor(name, list(shape), dt, kind="ExternalInput")

    x_d = DI("x", [S_, D_], F32)
    pT_d = DI("pT", [2, 256, S_], F32)
    pos_d = DI("pos", [128, NT], I32)
    ident_d = DI("ident", [128, 128], BF16)
    J_d = DI("J", [128, 128], F32)
    oh_d = DI("oh", [2, 33, 383], F32)
    invf_d = DI("invf", [128, 16], F32)
    relb_d = DI("rel_bias", [32, 12], F32)
    npre_d = DI("norm_pre", [2, D_], F32)
    npost_d = DI("norm_post", [2, D_], F32)
    win_d = DI("w_in", [2, D_, IN_TOTAL], F32)
    lam_d = DI("da_lambda", [2, 256], F32)
    subln_d = DI("da_subln", [2, 128], F32)
    qn_d = DI("mla_q_norm", [2, 256], F32)
    wqb_d = DI("mla_w_qb", [2, 256, 768], F32)
    kvn_d = DI("mla_kv_norm", [2, 128], F32)
    wkvb_d = DI("mla_w_kvb", [2, 128, 1024], F32)
    sink_d = DI("sw_sinks", [2, 8], F32)
    wbr_d = [DI("w_br_a", [2, 512, D_], F32), DI("w_br_b", [2, 512, D_], F32), DI("w_br_c", [2, 512, D_], F32)]
    wout_d = DI("w_out", [2, D_, D_], F32)
    wpg_d = DI("w_ple_gate", [2, D_, D_], F32)
    wpp_d = DI("w_ple_proj", [2, 256, D_], F32)
    out_d = nc.dram_tensor("out", [S_, D_], F32, kind="ExternalOutput")
    scr_d = nc.dram_tensor("scr", [12, 383], F32, kind="Internal")
    tap_d = {}
    for (nm, shp, dt) in taps:
        tap_d[nm] = nc.dram_tensor("tap_" + nm, list(shp), dt, kind="ExternalOutput")

    ARENA = 212736
    arena = nc.alloc_sbuf_tensor("arena", [128, ARENA], U8)
    top = [0]

    def alloc(shape, dt, nbytes_el):
        n = 1
        for s in shape[1:]:
            n *= s
        nb = n * nbytes_el
        nb = (nb + 63) // 64 * 64
        off = top[0]
        top[0] += nb
        assert top[0] <= ARENA, ("SBUF overflow", top[0])
        a = arena[:, off:off + n * nbytes_el].bitcast(dt)
        if len(shape) == 3:
            a = a.rearrange("p (a b) -> p a b", b=shape[2])
        elif len(shape) == 4:
            a = a.rearrange("p (a b c) -> p a b c", b=shape[2], c=shape[3])
        return a

    def f32(*shape): return alloc([128] + list(shape), F32, 4)
    def b16(*shape): return alloc([128] + list(shape), BF16, 2)
    def i32(*shape): return alloc([128] + list(shape), I32, 4)

    ps_t = nc.alloc_psum_tensor("ps", [128, 8, 512], F32)

    def PS(b, n=512, nb=1):
        if nb == 1:
            return ps_t[:, b, 0:n]
        return ps_t[:, b:b + nb, :]

    def PSB(b):
        return ps_t[:, b, :].bitcast(BF16)

    xres = f32(NT, D_)
    hT = b16(8, S_)
    uT = b16(12, S_)
    UT_BASE = top[0] - 12 * S_ * 2
    ident = b16(128)
    cosT = f32(NT, 16)
    sinT = f32(NT, 16)
    cbias = f32(12)
    maskT = f32(128)
    Jsb = f32(128)
    stat = f32(128)
    stati = i32(16)
    PERSIST_TOP = top[0]

    def dma(q, out, in_, reads=(), writes=(), stream=None):
        assert stream is not None
        return S.add(q, lambda h: h.dma_start(out=out, in_=in_), reads, writes, stream)

    def mm(out, lhsT, rhs, start, stop, reads, writes, skip=False):
        if skip:
            return S.add('pe', lambda h: h.matmul(out, lhsT, rhs, start=start, stop=stop, skip_group_check=True), reads, writes)
        return S.add('pe', lambda h: h.matmul(out, lhsT, rhs, start=start, stop=stop), reads, writes)

    def tr(out, in_, reads, writes):
        np_ = in_.shape[0]
        return S.add('pe', lambda h: h.transpose(out, in_, ident[0:np_, 0:np_]), reads, writes)

    def act(out, in_, func, reads, writes, bias=0.0, scale=1.0, accum=None):
        if accum is None:
            return S.add('act', lambda h: h.activation(out=out, in_=in_, func=func, bias=bias, scale=scale), reads, writes)
        return S.add('act', lambda h: h.activation(out=out, in_=in_, func=func, bias=bias, scale=scale, accum_out=accum), reads, writes)

    def cp(eng, out, in_, reads, writes):
        if eng == 'act':
            return S.add(eng, lambda h: h.activation(out=out, in_=in_, func=AF.Copy), reads, writes)
        return S.add(eng, lambda h: h.tensor_copy(out=out, in_=in_), reads, writes)

    def tt(eng, out, a, b, op, reads, writes):
        return S.add(eng, lambda h: h.tensor_tensor(out=out, in0=a, in1=b, op=op), reads, writes)

    def ts(eng, out, a, s1, s2, op0, op1, reads, writes):
        if s2 is None:
            return S.add(eng, lambda h: h.tensor_scalar(out=out, in0=a, scalar1=s1, scalar2=None, op0=op0), reads, writes)
        return S.add(eng, lambda h: h.tensor_scalar(out=out, in0=a, scalar1=s1, scalar2=s2, op0=op0, op1=op1), reads, writes)

    def stt(out, a, s, b, op0, op1, reads, writes, accum=None):
        if accum is None:
            return S.add('dve', lambda h: h.scalar_tensor_tensor(out=out, in0=a, scalar=s, in1=b, op0=op0, op1=op1), reads, writes)
        return S.add('dve', lambda h: h.scalar_tensor_tensor(out=out, in0=a, scalar=s, in1=b, op0=op0, op1=op1, accum_out=accum), reads, writes)

    def memset(eng, out, val, reads, writes):
        return S.add(eng, lambda h: h.memset(out, val), reads, writes)

    def recip(out, in_, reads, writes):
        return S.add('dve', lambda h: h.reciprocal(out=out, in_=in_), reads, writes)

    def bcast_row(handle, off, n):
        return AP(handle, off, [[0, 128], [1, n]])

    def rsqrt_newton(dst, src, n, scale, eps, rk, scratch_f, scratch_i, eng='dve'):
        v = scratch_f[:, 0:n]; y = dst; c = scratch_f[:, n:2 * n]
        ti = scratch_i[:, 0:n]
        R = rk
        ts('dve', v, src, scale, eps, ALU.mult, ALU.add, R, R)
        ts('dve', ti, v.bitcast(I32), 1, None, ALU.logical_shift_right, None, R, R)
        ts('dve', ti, ti, -1.0, float(0x5f3759df), ALU.mult, ALU.add, R, R)
        y0 = ti.bitcast(F32)
        for it in range(3):
            yin = y0 if it == 0 else y
            tt(eng, c, v, yin, ALU.mult, R, R)
            tt(eng, c, c, yin, ALU.mult, R, R)
            ts(eng, c, c, -0.5, 1.5, ALU.mult, ALU.add, R, R)
            tt(eng, y, yin, c, ALU.mult, R, R)

    wq_cnt = [0]

    def load_w(dst, src_handle, l_off, row_stride, r0, nkc, c0, ncols, reads=(), writes=(), stream=None):
        src = AP(src_handle, l_off + r0 * row_stride + c0, [[row_stride, 128], [128 * row_stride, nkc], [1, ncols]])
        return S.add('pool', lambda h: h.dma_start(out=dst, in_=src), reads, writes, stream)

    dma('sp', ident, ident_d.ap(), writes=['ident'], stream='c_ident')
    dma('sp', cbias, bcast_row(relb_d, 31 * 12, 12), writes=['cbias'], stream='c_cb')
    for tt_ in range(NT):
        dma('sp', xres[:, tt_, :], x_d.ap()[tt_ * 128:(tt_ + 1) * 128, :], writes=[('x', tt_)], stream='xin')

    setup_mark = top[0]
    tab33 = f32(12)
    ohsb = f32(2, 383)
    posi = i32(NT)
    posf = f32(NT)
    invf = f32(16)
    ang = f32(NT, 16)
    w1 = f32(NT, 16)
    w2 = f32(NT, 16)
    tvec = f32(383)
    dma('sp', tab33[0:32, :], relb_d.ap(), writes=['tab33'], stream='c_tab')
    memset('dve', tab33[32:33, :], NEG, [], ['tab33'])
    dma('sp', ohsb[0:33, :, :], oh_d.ap().rearrange("t b r -> b t r"), writes=['ohsb'], stream='c_oh')
    dma('sp', Jsb, J_d.ap(), writes=['Jsb'], stream='c_J')
    dma('sp', posi, pos_d.ap(), writes=['posi'], stream='c_pos')
    dma('sp', invf, invf_d.ap(), writes=['invf'], stream='c_invf')
    mm(PS(0, 383)[0:12, :], tab33[0:33, 0:12], ohsb[0:33, 0, :], True, True, ['tab33', 'ohsb'], [('ps', 0)])
    mm(PS(1, 383)[0:12, :], tab33[0:33, 0:12], ohsb[0:33, 1, :], True, True, ['tab33', 'ohsb'], [('ps', 1)])
    cp('dve', tvec[0:12, :], PS(1, 383)[0:12, :], [('ps', 1)], ['tvec'])
    cp('dve', tvec[0:4, :], PS(0, 383)[0:4, :], [('ps', 0), 'tvec'], ['tvec'])
    dma('sp', scr_d.ap(), tvec[0:12, :], reads=['tvec'], writes=['scr'], stream='c_scr')

    cp('dve', posf, posi, ['posi'], ['posf'])
    tt('dve', ang, posf.unsqueeze(2).broadcast_to([128, NT, 16]), invf.unsqueeze(1).broadcast_to([128, NT, 16]), ALU.mult,
       ['posf', 'invf'], ['ang'])
    MAGIC = 12582912.0
    C1 = 6.28125
    C2 = 2.0 * math.pi - 6.28125
    ts('dve', w1, ang, 1.0 / (2.0 * math.pi), MAGIC, ALU.mult, ALU.add, ['ang'], ['w1'])
    ts('dve', w1, w1, -MAGIC, None, ALU.add, None, ['w1'], ['w1'])
    stt(w2, w1, -C1, ang, ALU.mult, ALU.add, ['w1', 'ang'], ['w2'])
    stt(w2, w1, -C2, w2, ALU.mult, ALU.add, ['w1', 'w2'], ['w2'])
    halfpi = f32(1)
    memset('dve', halfpi, math.pi / 2.0, [], ['halfpi'])
    act(w1, w2, AF.Sin, ['w2'], ['w1'], scale=0.5)
    ts('dve', ang, w2, -1.0, None, ALU.mult, None, ['w2'], ['ang'])
    tt('dve', ang, ang, w2, ALU.max, ['ang', 'w2'], ['ang'])
    act(ang, ang, AF.Sin, ['ang', 'halfpi'], ['ang'], scale=-0.5, bias=halfpi[:, 0:1])
    stt(sinT, w1, 2.0, ang, ALU.mult, ALU.mult, ['w1', 'ang'], ['sinT'])
    tt('dve', w2, w1, w1, ALU.mult, ['w1'], ['w2'])
    ts('dve', cosT, w2, -2.0, 1.0, ALU.mult, ALU.add, ['w2'], ['cosT'])
    S.barrier()
    top[0] = setup_mark

    RK = ['stat']
    ptc = [0]
    sbc = [0]
    tsc = [0]

    def attn_block(qb, qT, kT, p0, Kd, v, vres, dv1, scale, span, bias_ap, cb_ap, band, ocol, PTs, tmpS, qkres):
        j_lo = 0 if band is None else max(0, 4 * qb - band)
        js = list(range(j_lo, 4 * qb + 4))
        info = {}

        def geom(j):
            i_lo = max(j, 4 * qb)
            i_hi = 4 * qb + 3 if band is None else min(4 * qb + 3, j + band)
            n_i = i_hi - i_lo + 1
            off = (i_lo - 4 * qb) * 128
            return i_lo, i_hi, n_i, off, n_i * 128

        def issue_S(j):
            i_lo, i_hi, n_i, off, ncols = geom(j)
            sb = 4 + (sbc[0] % 4); sbc[0] += 1
            info[j] = sb
            mm(PS(sb)[:, off:off + ncols], kT[p0:p0 + Kd, j * 128:(j + 1) * 128],
               qT[p0:p0 + Kd, qb * 512 + off:qb * 512 + off + ncols], True, True, qkres, [('ps', sb)])

        def issue_rest(j):
            i_lo, i_hi, n_i, off, ncols = geom(j)
            sb = info[j]
            k = ptc[0] % len(PTs); ptc[0] += 1
            pt = PTs[k]
            d_hi = min(i_hi, j + span - 1)
            nd = d_hi - i_lo + 1 if d_hi >= i_lo else 0
            if nd > 0:
                c0 = (i_lo - j) * 128
                tk = tsc[0] % len(tmpS); tsc[0] += 1
                tb_ = tmpS[tk]
                stt(tb_[:, 0:nd * 128], PS(sb)[:, off:off + nd * 128], scale, bias_ap[:, c0:c0 + nd * 128],
                    ALU.mult, ALU.add, [('ps', sb), 'biasT'], [('tmpS', tk)])
                act(pt[:, off:off + nd * 128], tb_[:, 0:nd * 128], AF.Exp, [('tmpS', tk)], [('pt', k)])
            if n_i > nd:
                o2 = off + nd * 128
                act(pt[:, o2:off + ncols], PS(sb)[:, o2:off + ncols], AF.Exp, [('ps', sb), 'cbias'], [('pt', k)],
                    scale=scale, bias=(cb_ap if cb_ap is not None else 0.0))
            for i in range(i_lo, i_hi + 1):
                mm(PS(i % 4)[:, ocol:ocol + dv1], pt[:, (i - 4 * qb) * 128:(i - 4 * qb + 1) * 128], v[:, j, :],
                   False, False, [('pt', k), vres, ('Oz', i % 4)], [('O', i % 4)], skip=True)

        SKEW = 3
        for idx in range(min(SKEW, len(js))):
            issue_S(js[idx])
        for idx, j in enumerate(js):
            if idx + SKEW < len(js):
                issue_S(js[idx + SKEW])
            issue_rest(j)

    def proj_fm(dst, dres, w, wres, M, src, sres_fn, nkc, rows=None, eng='dve', bank0=6):
        for tb in range(4):
            pb = bank0 + (tb % 2)
            for kc in range(nkc):
                rhs = src[:, kc, tb * 512:(tb + 1) * 512] if nkc > 1 or len(src.shape) == 3 else src[:, tb * 512:(tb + 1) * 512]
                lw = w[:, kc, :] if len(w.shape) == 3 else w
                mm(PS(pb)[0:M, :], lw, rhs, kc == 0, kc == nkc - 1, [wres, sres_fn(tb)], [('ps', pb)])
            r0, r1 = rows if rows is not None else (0, M)
            cp(eng, dst[r0:r1, tb * 512:(tb + 1) * 512], PS(pb)[r0:r1, :], [('ps', pb)], [dres])

    hTres = lambda tb: ('hT', tb)

    def sz_pair(WZ_, szp_, fTb_):
        for tg in range(4):
            pb = 6 + (tg % 2)
            kk = tg % 2
            for t4 in range(4):
                t = tg * 4 + t4
                for kc in range(8):
                    mm(PS(pb)[:, t4 * 128:(t4 + 1) * 128], hT[:, kc, t * 128:(t + 1) * 128], WZ_[:, kc, :], kc == 0, kc == 7,
                       ['WZ', ('hT', tg)], [('ps', pb)])
            act(fTb_[kk], PS(pb), AF.Tanh, [('ps', pb)], [('hn', kk)], scale=0.5)
            stt(szp_[:, tg * 4:(tg + 1) * 4, :].rearrange("p a b -> p (a b)"), fTb_[kk], 1.0, PS(pb), ALU.add, ALU.mult,
                [('hn', kk), ('ps', pb)], ['szp'])
    allhT = [('hT', i) for i in range(4)]

    for l in range(nl):
        lam_init = 0.8 - 0.6 * math.exp(-0.3 * l)
        mark_layer = top[0]
        hn = [b16(D_), b16(D_)]
        junk = b16(D_)
        mark_s0 = top[0]
        gpre = f32(D_)
        dma('sp', gpre, bcast_row(npre_d, l * D_, D_), writes=['gpre'], stream='g_pre')
        for t in range(NT):
            act(junk, xres[:, t, :], AF.Square, [('x', t), 'junk'], ['junk', 'stat'], accum=stat[:, t:t + 1])
        rsqrt_newton(stat[:, 16:32], stat[:, 0:16], 16, 1.0 / D_, EPS, RK, stat[:, 64:96], stati)
        for t in range(NT):
            b = t % 2
            stt(hn[b], xres[:, t, :], stat[:, 16 + t:17 + t], gpre, ALU.mult, ALU.mult,
                [('x', t), 'stat', 'gpre'], [('hn', b)])
            pb = 6 + b
            for kc in range(8):
                tr(PSB(pb)[:, kc * 128:(kc + 1) * 128], hn[b][:, kc * 128:(kc + 1) * 128], [('hn', b), 'ident'], [('ps', pb)])
            cp('act' if b else 'dve', hT[:, :, t * 128:(t + 1) * 128],
               PSB(pb).rearrange("p (a b) -> p a b", b=128), [('ps', pb)], [('hT', t // 4)])
        S.barrier()
        top[0] = mark_s0
        if 'hT' in tap_d and l == 0:
            dma('sp', tap_d['hT'].ap().rearrange("(kc p) t -> p kc t", p=128), hT, reads=allhT, stream='tap_hT')

        biasT = f32(12, 256)
        mark_mix = top[0]
        Thk = f32(12, 256)
        dma('sp', Thk, AP(scr_d, 0, [[1, 128], [383, 12], [1, 256]]), reads=['scr'], writes=['Thk'], stream='thk')
        for h in range(12):
            pb = 6 + (h % 2)
            mm(PS(pb)[:, 0:256], Jsb, Thk[:, h, :], True, True, ['Jsb', 'Thk'], [('ps', pb)])
            cp('dve', biasT[:, h, :], PS(pb)[:, 0:256], [('ps', pb)], ['biasT'])
        ts('dve', maskT, biasT[:, 0, 0:128], -1000.0, NEG, ALU.is_lt, ALU.mult, ['biasT'], ['biasT'])
        S.barrier()
        top[0] = mark_mix
        if 'biasT' in tap_d and l == 0:
            dma('sp', tap_d['biasT'].ap().rearrange("p (h c) -> p h c", c=256), biasT, reads=['biasT'], stream='tap_bT')
        if stop_after == 'bias':
            break

        PTs = [b16(512), b16(512), b16(512), b16(512)]
        tmpS = [f32(256), f32(256)]
        fTs = [f32(128), f32(128)]
        fZs = None
        fAs = [f32(128), f32(128)]
        rrs = [f32(4), f32(4)]
        fT = fTs[0]
        mark_mix = top[0]

        lamb = f32(256)
        gsub = f32(128)
        WAs = [uT[:, 4 + 2 * i:6 + 2 * i, :].rearrange("p a t -> p (a t)").rearrange("p (k w c) -> p k w c", w=4, c=128)
               for i in range(2)]

        def load_WA(hh):
            for wi, key in enumerate(('a_q', 'a_k', 'a_v', 'a_z')):
                load_w(WAs[hh % 2][:, :, wi, :], win_d, l * D_ * IN_TOTAL, IN_TOTAL, 0, 8, OFF[key] + hh * 128, 128,
                       writes=[('WA', hh % 2)], stream=('WA', hh % 2))
        load_WA(0)
        qT_h = b16(S_)
        kT_h = b16(S_)
        v_h = b16(NT, 129)
        Dm = f32(NT, 128)
        OcpA = [f32(4, 258), f32(4, 258)]
        fT4s = [uT[:, 8, 0:1024].bitcast(F32), uT[:, 8, 1024:2048].bitcast(F32)]
        fZ4s = [uT[:, 9, 0:1024].bitcast(F32), uT[:, 9, 1024:2048].bitcast(F32)]
        dma('sp', lamb, bcast_row(lam_d, l * 256, 256), writes=['lamb'], stream='lamb')
        dma('sp', gsub, bcast_row(subln_d, l * 128, 128), writes=['gsub'], stream='gsub')
        stt(fT[:, 0:64], lamb[:, 0:64], 1.0, lamb[:, 64:128], ALU.mult, ALU.mult, ['lamb'], [('fT', 0), 'stat'], accum=stat[:, 36:37])
        stt(fT[:, 64:128], lamb[:, 128:192], 1.0, lamb[:, 192:256], ALU.mult, ALU.mult, ['lamb'], [('fT', 0), 'stat'], accum=stat[:, 37:38])
        act(stat[:, 38:40], stat[:, 36:38], AF.Exp, ['stat'], ['stat'])
        tt('dve', stat[:, 40:41], stat[:, 39:40], stat[:, 38:39], ALU.subtract, ['stat'], ['stat'])
        ts('dve', stat[:, 40:41], stat[:, 40:41], -lam_init, None, ALU.add, None, ['stat'], ['stat'])
        ts('dve', gsub, gsub, 0.5 * (1.0 - lam_init), None, ALU.mult, None, ['gsub'], ['gsub'])
        memset('dve', v_h[:, :, 128:129], 1.0, [], ['v_h'])
        def projA(hh):
            WA_ = WAs[hh % 2]
            WAr_ = ('WA', hh % 2)
            proj_fm(qT_h, 'qT_h', WA_[:, :, 0, :], WAr_, 128, hT, hTres, 8)
            proj_fm(kT_h, 'kT_h', WA_[:, :, 1, :], WAr_, 128, hT, hTres, 8, eng='act')
            for tg in range(4):
                pb = 6 + (tg % 2)
                for t4 in range(4):
                    t = tg * 4 + t4
                    for kc in range(8):
                        mm(PS(pb)[:, t4 * 128:(t4 + 1) * 128], hT[:, kc, t * 128:(t + 1) * 128], WA_[:, kc, 2, :],
                           kc == 0, kc == 7, [WAr_, ('hT', tg)], [('ps', pb)])
                cp('dve', v_h[:, tg * 4:(tg + 1) * 4, 0:128], PS(pb).rearrange("p (a b) -> p a b", b=128), [('ps', pb)], ['v_h'])
        projA(0)
        for h in range(4):
            WA = WAs[h % 2]
            WAr = ('WA', h % 2)
            if h + 1 < 4:
                load_WA(h + 1)
            allO = [('O', i_) for i_ in range(4)]
            allOz = [('Oz', i_) for i_ in range(4)]
            memset('dve', ps_t[:, 0:4, 0:258], 0.0, [], allO + allOz)
            for qb in range(4):
                for m in range(2):
                    attn_block(qb, qT_h, kT_h, m * 64, 64, v_h, 'v_h', 129, 0.125, 2, biasT[:, h, :], cbias[:, h:h + 1],
                               None, m * 129, PTs, tmpS, ['qT_h', 'kT_h'])
                oc = OcpA[qb % 2]
                ocr = ('OcpA', qb % 2)
                cp('dve', oc, ps_t[:, 0:4, 0:258], allO, [ocr])
                if qb < 3:
                    memset('dve', ps_t[:, 0:4, 0:258], 0.0, [], allO + allOz)
                for i4 in range(4):
                    t = qb * 4 + i4
                    O = oc[:, i4, :]
                    Ov = O.rearrange("p (m c) -> p m c", c=129)
                    kk = i4 % 2
                    rr = rrs[kk]; fA = fAs[kk]
                    recip(rr[:, 0:2], Ov[:, :, 128], [ocr], [('rr', kk)])
                    tt('dve', rr[:, 1:2], rr[:, 1:2], stat[:, 40:41], ALU.mult, [('rr', kk), 'stat'], [('rr', kk)])
                    ts('dve', fA, O[:, 129:257], rr[:, 1:2], None, ALU.mult, None, [ocr, ('rr', kk)], [('fA', kk)])
                    stt(Dm[:, t, :], O[:, 0:128], rr[:, 0:1], fA, ALU.mult, ALU.add, [ocr, ('rr', kk), ('fA', kk)], ['Dm'])
                    stt(junk[:, 0:128], Dm[:, t, :], 1.0, Dm[:, t, :], ALU.mult, ALU.mult, ['Dm', 'junk'], ['junk', 'stat'],
                        accum=stat[:, t:t + 1])
            if h + 1 < 4:
                projA(h + 1)
            rsqrt_newton(stat[:, 16:32], stat[:, 0:16], 16, 1.0 / 128.0, EPS, RK, stat[:, 64:96], stati)
            for tg in range(4):
                pb = 6 + (tg % 2)
                kk = tg % 2
                for t4 in range(4):
                    t = tg * 4 + t4
                    for kc in range(8):
                        mm(PS(pb)[:, t4 * 128:(t4 + 1) * 128], hT[:, kc, t * 128:(t + 1) * 128], WA[:, kc, 3, :], kc == 0, kc == 7,
                           [WAr, ('hT', tg)], [('ps', pb)])
                fT4 = fT4s[kk]; fZ4 = fZ4s[kk]
                act(fT4, PS(pb), AF.Tanh, [('ps', pb)], [('fT4', kk)], scale=0.5)
                stt(fZ4, fT4, 1.0, PS(pb), ALU.add, ALU.mult, [('fT4', kk), ('ps', pb)], [('fZ4', kk)])
                D4 = Dm[:, tg * 4:(tg + 1) * 4, :]
                f3 = fT4.rearrange("p (a b) -> p a b", b=128)
                tt('dve', f3, D4, stat[:, 16 + tg * 4:20 + tg * 4].unsqueeze(2).broadcast_to([128, 4, 128]), ALU.mult,
                   ['Dm', 'stat', ('fZ4', kk)], [('fT4', kk)])
                tt('dve', f3, f3, gsub.unsqueeze(1).broadcast_to([128, 4, 128]), ALU.mult, [('fT4', kk), 'gsub'], [('fT4', kk)])
                ub = hn[kk]
                tt('dve', ub[:, 0:512], fT4, fZ4, ALU.mult, [('fT4', kk), ('fZ4', kk)], [('hn', kk)])
                for t4 in range(4):
                    tr(PSB(pb)[:, t4 * 128:(t4 + 1) * 128], ub[:, t4 * 128:(t4 + 1) * 128], [('hn', kk), 'ident'], [('ps', pb)])
                cp('act', uT[:, h, tg * 512:(tg + 1) * 512], PSB(pb)[:, 0:512], [('ps', pb)], ['uT'])
        S.barrier()
        top[0] = mark_mix
        if 'uA' in tap_d and l == 0:
            dma('sp', tap_d['uA'].ap().rearrange("(kc p) t -> p kc t", p=128), uT[:, 0:4, :], reads=['uT'], stream='tap_uA')
        if stop_after == 'A':
            break

        gq = f32(256)
        gkv = f32(128)
        wqb = b16(2, 768)
        wkvb = b16(1024)
        qlatT = uT[:, 8:10, :]
        kvlatT = uT[:, 10, :]
        kropeT = uT[:, 11, :]
        WZ = b16(8, 128)
        qT_h = b16(S_)
        kT_h = b16(S_)
        vm_h = b16(NT, 65)
        q_tm = b16(4, 96)
        u_pair = b16(NT, 128)
        fTb = [hn[0].bitcast(F32), hn[1].bitcast(F32)]
        krt = b16(96)
        r6 = [f32(4, 16) for _ in range(6)]
        mark_b = top[0]
        Wlat = b16(8, 416)
        dma('sp', gq, bcast_row(qn_d, l * 256, 256), writes=['gq'], stream='gq')
        dma('sp', gkv, bcast_row(kvn_d, l * 128, 128), writes=['gkv'], stream='gkv')
        load_w(Wlat, win_d, l * D_ * IN_TOTAL, IN_TOTAL, 0, 8, OFF['b_lat'], 416, writes=['Wlat'], stream='Wlat')
        load_w(wqb, wqb_d, l * 256 * 768, 768, 0, 2, 0, 768, writes=['wqb'], stream='wqb')
        S.add('pool', lambda h_, o=wkvb, i=AP(wkvb_d, l * 128 * 1024, [[1024, 128], [1, 1024]]): h_.dma_start(out=o, in_=i),
              (), ['wkvb'], 'wkvb')
        memset('dve', krt[:, 0:64], 0.0, [], ['krt'])
        memset('dve', vm_h[:, :, 64:65], 2.0, [], ['vm_h'])
        for t in range(NT):
            pb = 6 + (t % 2)
            for kc in range(8):
                mm(PS(pb)[:, 0:416], hT[:, kc, t * 128:(t + 1) * 128], Wlat[:, kc, :], kc == 0, kc == 7,
                   ['Wlat', ('hT', t // 4)], [('ps', pb)])
            act(junk[:, 0:256], PS(pb)[:, 0:256], AF.Square, [('ps', pb), 'junk'], ['junk', 'stat'], accum=stat[:, t:t + 1])
            act(junk[:, 256:384], PS(pb)[:, 256:384], AF.Square, [('ps', pb), 'junk'], ['junk', 'stat'], accum=stat[:, 42 + t:43 + t])
        rsqrt_newton(stat[:, 16:32], stat[:, 0:16], 16, 1.0 / 256.0, EPS, RK, stat[:, 64:96], stati)
        rkv = rr
        rkv = f32(16)
        rsqrt_newton(rkv, stat[:, 42:58], 16, 1.0 / 128.0, EPS, RK + ['rkv'], f32(32), stati)
        for t in range(NT):
            pb = 6 + (t % 2)
            b = t % 2
            for kc in range(8):
                mm(PS(pb)[:, 0:416], hT[:, kc, t * 128:(t + 1) * 128], Wlat[:, kc, :], kc == 0, kc == 7,
                   ['Wlat', ('hT', t // 4)], [('ps', pb)])
            stt(hn[b][:, 0:256], PS(pb)[:, 0:256], stat[:, 16 + t:17 + t], gq, ALU.mult, ALU.mult,
                [('ps', pb), 'stat', 'gq'], [('hn', b)])
            stt(hn[b][:, 256:384], PS(pb)[:, 256:384], rkv[:, t:t + 1], gkv, ALU.mult, ALU.mult,
                [('ps', pb), 'stat', 'rkv', 'gkv'], [('hn', b)])
            x1 = PS(pb)[:, 384:400]; x2 = PS(pb)[:, 400:416]
            cs = cosT[:, t, :]; sn = sinT[:, t, :]
            a_, b_, c_, d_ = r6[0][:, 0, :], r6[1][:, 0, :], r6[2][:, 0, :], r6[3][:, 0, :]
            tt('dve', a_, x1, cs, ALU.mult, [('ps', pb), 'cosT'], ['r6'])
            tt('dve', b_, x2, sn, ALU.mult, [('ps', pb), 'sinT'], ['r6'])
            tt('dve', c_, x2, cs, ALU.mult, [('ps', pb), 'cosT'], ['r6'])
            tt('dve', d_, x1, sn, ALU.mult, [('ps', pb), 'sinT'], ['r6'])
            tt('dve', krt[:, 64:80], a_, b_, ALU.subtract, ['r6'], ['krt'])
            tt('dve', krt[:, 80:96], c_, d_, ALU.add, ['r6'], ['krt'])
            pt_ = 4 + (t % 2)
            tr(PSB(pt_)[:, 0:128], hn[b][:, 0:128], [('hn', b), 'ident'], [('ps', pt_)])
            tr(PSB(pt_)[:, 128:256], hn[b][:, 128:256], [('hn', b), 'ident'], [('ps', pt_)])
            tr(PSB(pt_)[:, 256:384], hn[b][:, 256:384], [('hn', b), 'ident'], [('ps', pt_)])
            tr(PSB(pt_)[0:96, 384:512], krt[:, 0:96], ['krt', 'ident'], [('ps', pt_)])
            cp('act', qlatT[:, :, t * 128:(t + 1) * 128], PSB(pt_)[:, 0:256].rearrange("p (a b) -> p a b", b=128),
               [('ps', pt_)], ['qlatT'])
            cp('act', kvlatT[:, t * 128:(t + 1) * 128], PSB(pt_)[:, 256:384], [('ps', pt_)], ['kvlatT'])
            cp('act', kropeT[64:96, t * 128:(t + 1) * 128], PSB(pt_)[64:96, 384:512], [('ps', pt_)], ['kropeT'])
        S.barrier()
        top[0] = mark_b
        szp = uT[:, 7, :].rearrange("p (a b) -> p a b", b=128)
        Ocp = [f32(4, 65), f32(4, 65)]
        if stop_after == 'B0':
            break
        for h in range(8):
            if stop_after == 'B1' and h == 1:
                break
            if h % 2 == 0:
                load_w(WZ, win_d, l * D_ * IN_TOTAL, IN_TOTAL, 0, 8, OFF['b_z'] + h * 64, 128, writes=['WZ'], stream='WZ')
                sz_pair(WZ, szp, fTb)
            for tb in range(4):
                pb = 6 + (tb % 2)
                for t4 in range(4):
                    t = tb * 4 + t4
                    for kc in range(2):
                        mm(PS(pb)[:, t4 * 96:(t4 + 1) * 96], qlatT[:, kc, t * 128:(t + 1) * 128], wqb[:, kc, h * 96:(h + 1) * 96],
                           kc == 0, kc == 1, ['qlatT', 'wqb'], [('ps', pb)])
                import os as _os
                QS = int(_os.environ.get("QSTEP", "9"))
                P3 = PS(pb)[:, 0:384].rearrange("p (a b) -> p a b", b=96)
                x1 = P3[:, :, 64:80]; x2 = P3[:, :, 80:96]
                cs = cosT[:, tb * 4:(tb + 1) * 4, :]; sn = sinT[:, tb * 4:(tb + 1) * 4, :]
                if QS >= 1:
                    tt('dve', r6[0], x1, cs, ALU.mult, [('ps', pb), 'cosT'], ['r6'])
                    tt('dve', r6[1], x2, sn, ALU.mult, [('ps', pb), 'sinT'], ['r6'])
                    tt('dve', r6[2], x2, cs, ALU.mult, [('ps', pb), 'cosT'], ['r6'])
                    tt('dve', r6[3], x1, sn, ALU.mult, [('ps', pb), 'sinT'], ['r6'])
                if QS >= 2:
                    tt('dve', q_tm[:, :, 64:80], r6[0], r6[1], ALU.subtract, ['r6'], ['q_tm'])
                    tt('dve', q_tm[:, :, 80:96], r6[2], r6[3], ALU.add, ['r6'], ['q_tm'])
                if QS >= 3:
                    cp('dve', q_tm[:, :, 0:64], P3[:, :, 0:64], [('ps', pb)], ['q_tm'])
                pt_ = 4 + (tb % 2)
                if QS >= 4:
                    for t4 in range(4):
                        tr(PSB(pt_)[0:96, t4 * 128:(t4 + 1) * 128], q_tm[:, t4, :], ['q_tm', 'ident'], [('ps', pt_)])
                if QS >= 5:
                    cp('dve', qT_h[0:96, tb * 512:(tb + 1) * 512], PSB(pt_)[0:96, 0:512], [('ps', pt_)], ['qT_h'])
                if QS < 9:
                    break
            if stop_after == 'B1a':
                break
            proj_fm(kT_h, 'kT_h', wkvb[:, h * 128:h * 128 + 64], 'wkvb', 64, kvlatT, lambda tb: 'kvlatT', 1, eng='act')
            cp('pool', kT_h[64:96, :], kropeT[64:96, :], ['kropeT'], ['kT_h'])
            if stop_after == 'B1b':
                break
            for tg in range(2):
                pb = 6 + (tg % 2)
                for t8 in range(8):
                    t = tg * 8 + t8
                    mm(PS(pb)[:, t8 * 64:(t8 + 1) * 64], kvlatT[:, t * 128:(t + 1) * 128], wkvb[:, h * 128 + 64:h * 128 + 128],
                       True, True, ['kvlatT', 'wkvb'], [('ps', pb)])
                cp('dve', vm_h[:, tg * 8:(tg + 1) * 8, 0:64], PS(pb).rearrange("p (a b) -> p a b", b=64), [('ps', pb)], ['vm_h'])
            allO = [('O', i_) for i_ in range(4)]
            allOz = [('Oz', i_) for i_ in range(4)]
            zc = (h % 2) * 64
            memset('dve', ps_t[:, 0:4, 0:65], 0.0, [], allO + allOz)
            for qb in range(4):
                attn_block(qb, qT_h, kT_h, 0, 96, vm_h, 'vm_h', 65, 96.0 ** -0.5, 1, maskT, None, None, 0, PTs, tmpS,
                           ['qT_h', 'kT_h'])
                oc = Ocp[qb % 2]
                ocr = ('Ocp', qb % 2)
                cp('dve', oc, ps_t[:, 0:4, 0:65], allO, [ocr])
                if qb < 3:
                    memset('dve', ps_t[:, 0:4, 0:65], 0.0, [], allO + allOz)
                rr = rrs[qb % 2]
                recip(rr[:, 0:4], oc[:, :, 64], [ocr], [('rr', qb % 2)])
                for i4 in range(4):
                    t = qb * 4 + i4
                    stt(u_pair[:, t, zc:zc + 64], oc[:, i4, 0:64], rr[:, i4:i4 + 1], szp[:, t, zc:zc + 64], ALU.mult, ALU.mult,
                        [ocr, ('rr', qb % 2), 'szp'], ['u_pair'])
            if h % 2 == 1:
                for t in range(NT):
                    pb = 6 + (t % 2)
                    tr(PSB(pb)[:, 512:640], u_pair[:, t, :], ['u_pair', 'ident'], [('ps', pb)])
                    cp('act', uT[:, 4 + h // 2, t * 128:(t + 1) * 128], PSB(pb)[:, 512:640], [('ps', pb)], ['uT'])
        if stop_after in ('B1', 'B1a', 'B1b', 'B1c'):
            break
        S.barrier()
        top[0] = mark_mix
        if 'uB' in tap_d and l == 0:
            dma('sp', tap_d['uB'].ap().rearrange("(kc p) t -> p kc t", p=128), uT[:, 4:8, :], reads=['uT'], stream='tap_uB')
        if stop_after == 'B':
            break

        WQ = b16(8, 4, 128)
        WKV = b16(8, 256)
        WZ = b16(8, 128)
        ckT = b16(S_)
        cv = b16(NT, 2, 65)
        qT_h = b16(S_)
        u_pair = b16(NT, 128)
        szp = uT[:, 11, :].rearrange("p (a b) -> p a b", b=128)
        Ocp = [f32(4, 65), f32(4, 65)]
        fTb = [hn[0].bitcast(F32), hn[1].bitcast(F32)]
        sk = f32(8)
        dma('sp', sk, bcast_row(sink_d, l * 8, 8), writes=['sk'], stream='sk')
        act(sk, sk, AF.Exp, ['sk'], ['sk'])
        ts('dve', sk, sk, 2.0, None, ALU.mult, None, ['sk'], ['sk'])
        for kvh in range(2):
            for g in range(4):
                load_w(WQ[:, :, g, kvh * 64:(kvh + 1) * 64], win_d, l * D_ * IN_TOTAL, IN_TOTAL, 0, 8,
                       OFF['c_q'] + kvh * 256 + g * 64, 64, writes=['WQ'], stream='WQ')
        load_w(WKV, win_d, l * D_ * IN_TOTAL, IN_TOTAL, 0, 8, OFF['c_k'], 256, writes=['WKV'], stream='WKV')
        memset('dve', cv[:, :, :, 64:65], 2.0, [], ['cv'])
        proj_fm(ckT, 'ckT', WKV[:, :, 0:128], 'WKV', 128, hT, hTres, 8)
        for tg in range(4):
            pb = 6 + (tg % 2)
            for t4 in range(4):
                t = tg * 4 + t4
                for kc in range(8):
                    mm(PS(pb)[:, t4 * 128:(t4 + 1) * 128], hT[:, kc, t * 128:(t + 1) * 128], WKV[:, kc, 128:256],
                       kc == 0, kc == 7, ['WKV', ('hT', tg)], [('ps', pb)])
            cp('dve', cv[:, tg * 4:(tg + 1) * 4, :, 0:64], PS(pb).rearrange("p (a b c) -> p a b c", b=2, c=64), [('ps', pb)], ['cv'])
        for hq in range(8):
            kvh, g = hq // 4, hq % 4
            if hq % 2 == 0:
                load_w(WZ, win_d, l * D_ * IN_TOTAL, IN_TOTAL, 0, 8, OFF['c_z'] + hq * 64, 128, writes=['WZ'], stream='WZ')
                sz_pair(WZ, szp, fTb)
            proj_fm(qT_h, 'qT_h', WQ[:, :, g, :], 'WQ', 128, hT, hTres, 8, rows=(kvh * 64, kvh * 64 + 64))
            allO = [('O', i_) for i_ in range(4)]
            allOz = [('Oz', i_) for i_ in range(4)]
            zc = (hq % 2) * 64
            memset('dve', ps_t[:, 0:4, 0:65], 0.0, [], allO + allOz)
            for qb in range(4):
                attn_block(qb, qT_h, ckT, kvh * 64, 64, cv[:, :, kvh, :], 'cv', 65, 0.125, 2, biasT[:, 4 + hq, :], None, 1, 0,
                           PTs, tmpS, ['qT_h', 'ckT'])
                oc = Ocp[qb % 2]
                ocr = ('Ocp', qb % 2)
                cp('dve', oc, ps_t[:, 0:4, 0:65], allO, [ocr])
                if qb < 3:
                    memset('dve', ps_t[:, 0:4, 0:65], 0.0, [], allO + allOz)
                rr = rrs[qb % 2]
                ts('dve', rr[:, 0:4], oc[:, :, 64], sk[:, hq:hq + 1], None, ALU.add, None, [ocr, 'sk'], [('rr', qb % 2)])
                recip(rr[:, 0:4], rr[:, 0:4], [('rr', qb % 2)], [('rr', qb % 2)])
                for i4 in range(4):
                    t = qb * 4 + i4
                    stt(u_pair[:, t, zc:zc + 64], oc[:, i4, 0:64], rr[:, i4:i4 + 1], szp[:, t, zc:zc + 64], ALU.mult, ALU.mult,
                        [ocr, ('rr', qb % 2), 'szp'], ['u_pair'])
            if hq % 2 == 1:
                for t in range(NT):
                    pb = 6 + (t % 2)
                    tr(PSB(pb)[:, 512:640], u_pair[:, t, :], ['u_pair', 'ident'], [('ps', pb)])
                    cp('act', uT[:, 8 + hq // 2, t * 128:(t + 1) * 128], PSB(pb)[:, 512:640], [('ps', pb)], ['uT'])
        S.barrier()
        top[0] = mark_s0
        if 'uC' in tap_d and l == 0:
            dma('sp', tap_d['uC'].ap().rearrange("(kc p) t -> p kc t", p=128), uT[:, 8:12, :], reads=['uT'], stream='tap_uC')
        if stop_after == 'C':
            break

        y2buf = b16(7, S_)
        mark_m2 = top[0]
        WG = [b16(8, 3, 128), b16(8, 3, 128)]
        WB = [b16(4, 3, 128), b16(4, 3, 128)]
        Tsb = [hn[0].bitcast(F32), hn[1].bitcast(F32)]
        acc = f32(512)
        tmpm = junk.bitcast(F32)
        def load_M1(cc):
            wb_ = cc % 2
            for br in range(3):
                load_w(WG[wb_][:, :, br, :], win_d, l * D_ * IN_TOTAL, IN_TOTAL, 0, 8, OFF['g'] + br * 1024 + cc * 128, 128,
                       writes=[('WG', wb_)], stream=('WG', wb_))
                load_w(WB[wb_][:, :, br, :], wbr_d[br], l * 512 * D_, D_, 0, 4, cc * 128, 128,
                       writes=[('WB', wb_)], stream=('WB', wb_))
        load_M1(0)
        for c in range(8):
            wb = c % 2
            if c + 1 < 8:
                load_M1(c + 1)
            for tb in range(4):
                for br in range(3):
                    pg = 2 * ((tb * 3 + br) % 4)
                    pbk = pg + 1
                    for kc in range(8):
                        mm(PS(pg), WG[wb][:, kc, br, :], hT[:, kc, tb * 512:(tb + 1) * 512], kc == 0, kc == 7,
                           [('WG', wb), ('hT', tb)], [('ps', pg)])
                    for kc in range(4):
                        mm(PS(pbk), WB[wb][:, kc, br, :], uT[:, br * 4 + kc, tb * 512:(tb + 1) * 512], kc == 0, kc == 3,
                           [('WB', wb), 'uT'], [('ps', pbk)])
                    k = (tb * 3 + br) % 2
                    act(Tsb[k], PS(pg), AF.Tanh, [('ps', pg)], [('Tsb', k)], scale=0.5)
                    if br == 0:
                        stt(acc, Tsb[k], 1.0, PS(pbk), ALU.add, ALU.mult, [('Tsb', k), ('ps', pbk)], ['acc'])
                    elif br == 1:
                        stt(tmpm, Tsb[k], 1.0, PS(pbk), ALU.add, ALU.mult, [('Tsb', k), ('ps', pbk)], ['tmpm'])
                        tt('dve', acc, acc, tmpm, ALU.add, ['acc', 'tmpm'], ['acc'])
                    else:
                        stt(tmpm, Tsb[k], 1.0, PS(pbk), ALU.add, ALU.mult, [('Tsb', k), ('ps', pbk)], ['tmpm'])
                        if c < 7:
                            tt('dve', y2buf[:, c, tb * 512:(tb + 1) * 512], acc, tmpm, ALU.add, ['acc', 'tmpm'], ['y2'])
                        else:
                            tt('dve', hT[:, 0, tb * 512:(tb + 1) * 512], acc, tmpm, ALU.add, ['acc', 'tmpm'], [('hT', tb), 'y2'])
        S.barrier()
        top[0] = mark_m2
        if 'y2' in tap_d and l == 0:
            dma('sp', tap_d['y2'].ap()[0:896, :].rearrange("(kc p) t -> p kc t", p=128), y2buf, reads=['y2'], stream='tap_y2')
            dma('sp', tap_d['y2'].ap()[896:1024, :], hT[:, 0, :], reads=['y2'], stream='tap_y2')
        if stop_after == 'M1':
            break

        def ualloc(nbytes):
            o = ualloc.off; ualloc.off += nbytes
            assert ualloc.off <= UT_BASE + 12 * S_ * 2
            return o
        ualloc.off = UT_BASE

        def uview(shape, dt, el):
            n = 1
            for s_ in shape:
                n *= s_
            o = ualloc(n * el)
            a = arena[:, o:o + n * el].bitcast(dt)
            if len(shape) == 2:
                a = a.rearrange("p (a b) -> p a b", b=shape[1])
            return a
        wout = uview([8, D_], BF16, 2)
        wpg = uview([8, D_], BF16, 2)
        wpp = uview([2, D_], BF16, 2)
        pTl = uview([2, S_], BF16, 2)
        gpost = uview([D_], F32, 4)
        Tg = f32(D_)
        t1 = f32(D_)
        x1b = b16(D_)
        x1T = b16(8, 128)
        for half in range(2):
            load_w(wout[:, :, half * 512:(half + 1) * 512], wout_d, l * D_ * D_, D_, 0, 8, half * 512, 512, writes=['wout'], stream='wout')
        for half in range(2):
            load_w(wpg[:, :, half * 512:(half + 1) * 512], wpg_d, l * D_ * D_, D_, 0, 8, half * 512, 512, writes=['wpg'], stream='wpg')
        load_w(wpp[:, :, 0:512], wpp_d, l * 256 * D_, D_, 0, 2, 0, 512, writes=['wpp'], stream='wpp')
        load_w(wpp[:, :, 512:1024], wpp_d, l * 256 * D_, D_, 0, 2, 512, 512, writes=['wpp'], stream='wpp')
        for half in range(2):
            S.add('pool', lambda h_, o=pTl[:, :, half * 1024:(half + 1) * 1024],
                  i=AP(pT_d, l * 256 * S_ + half * 1024, [[S_, 128], [128 * S_, 2], [1, 1024]]): h_.dma_start(out=o, in_=i),
                  (), ['pTl'], 'pTl')
        dma('sp', gpost, bcast_row(npost_d, l * D_, D_), writes=['gpost'], stream='gpost')
        Vgs = [t1, Tg, f32(D_)]
        x1T7 = b16(S_)
        hn3 = [hn[0], hn[1], b16(D_)]

        def m2a_V(t):
            r = t % 3
            vb = 2 * r
            V2 = ps_t[:, vb:vb + 2, :].rearrange("p a b -> p (a b)")
            for half in range(2):
                for kc in range(8):
                    lhs = y2buf[:, kc, t * 128:(t + 1) * 128] if kc < 7 else hT[:, 0, t * 128:(t + 1) * 128]
                    mm(PS(vb + half), lhs, wout[:, kc, half * 512:(half + 1) * 512], kc == 0, kc == 7, ['y2', 'wout'],
                       [('ps', vb + half)])
            R = [('st2', r)]
            act(junk, V2, AF.Square, [('ps', vb), ('ps', vb + 1), 'junk'], ['junk'] + R, accum=stat[:, t:t + 1])
            stt(Vgs[r], V2, 1.0, gpost, ALU.mult, ALU.mult, [('ps', vb), ('ps', vb + 1), 'gpost'] + R, [('t1', r)])
            rsqrt_newton(stat[:, 16 + t:17 + t], stat[:, t:t + 1], 1, 1.0 / D_, 4.0 * EPS, R, stat[:, 64 + 2 * r:66 + 2 * r],
                         stati[:, r:r + 1], eng='pool')
            stt(xres[:, t, :], Vgs[r], stat[:, 16 + t:17 + t], xres[:, t, :], ALU.mult, ALU.add, [('t1', r), ('x', t)] + R, [('x', t)])
            q3 = t % 3
            cp('act', hn3[q3], xres[:, t, :], [('x', t)], [('hn3', q3)])

        def m2a_T(t):
            p = t % 2
            pb = 6 + p
            for kc in range(8):
                tr(PSB(pb)[:, kc * 128:(kc + 1) * 128], hn3[t % 3][:, kc * 128:(kc + 1) * 128], [('hn3', t % 3), 'ident'], [('ps', pb)])
            Pv = PSB(pb).rearrange("p (a b) -> p a b", b=128)
            cp('dve', hT[:, 1:8, t * 128:(t + 1) * 128], Pv[:, 0:7, :], [('ps', pb)], ['x1T'])
            cp('dve', x1T7[:, t * 128:(t + 1) * 128], Pv[:, 7, :], [('ps', pb)], ['x1T'])

        m2a_V(0)
        m2a_V(1)
        for t in range(NT):
            if t + 2 < NT:
                m2a_V(t + 2)
            m2a_T(t)

        for t in range(NT):
            p = t % 2
            gb = 4 * p
            G2 = ps_t[:, gb:gb + 2, :].rearrange("p a b -> p (a b)")
            P2 = ps_t[:, gb + 2:gb + 4, :].rearrange("p a b -> p (a b)")
            for half in range(2):
                for kc in range(8):
                    lhs = hT[:, 1 + kc, t * 128:(t + 1) * 128] if kc < 7 else x1T7[:, t * 128:(t + 1) * 128]
                    mm(PS(gb + half), lhs, wpg[:, kc, half * 512:(half + 1) * 512], kc == 0, kc == 7, ['x1T', 'wpg'],
                       [('ps', gb + half)])
                for kc in range(2):
                    mm(PS(gb + 2 + half), pTl[:, kc, t * 128:(t + 1) * 128], wpp[:, kc, half * 512:(half + 1) * 512], kc == 0, kc == 1,
                       ['pTl', 'wpp'], [('ps', gb + 2 + half)])
            act(Vgs[p], G2, AF.Tanh, [('ps', gb), ('ps', gb + 1)], [('t1', p)], scale=0.5)
            stt(Vgs[p], Vgs[p], 1.0, P2, ALU.add, ALU.mult, [('t1', p), ('ps', gb + 2), ('ps', gb + 3)], [('t1', p)])
            stt(xres[:, t, :], Vgs[p], 0.5, xres[:, t, :], ALU.mult, ALU.add, [('t1', p), ('x', t)], [('x', t)])
            if l == nl - 1:
                dma('sp', out_d.ap()[t * 128:(t + 1) * 128, :], xres[:, t, :], reads=[('x', t)], stream='out')
        S.barrier()
        top[0] = mark_layer

    fin_ops = []
    for k, v in S.streams.items():
        if str(k).startswith('out') or str(k).startswith('tap'):
            fin_ops.append(v[-1])
    op = S.add('sp', None)
    op.deps = set(fin_ops)

    sems = []

    def sem_alloc(name):
        s = nc.alloc_semaphore(name.replace("(", "_").replace(")", "_").replace(",", "_").replace("'", "").replace(" ", ""))
        sems.append(s)
        return s

    S.finalize(sem_alloc)
    with nc.Block() as block:
        @block.tensor
        def _(h): S.emit('pe', h)

        @block.scalar
        def _(h): S.emit('act', h)

        @block.vector
        def _(h): S.emit('dve', h)

        @block.gpsimd
        def _(h): S.emit('pool', h)

        @block.sync
        def _(h): S.emit('sp', h)
    return nc


def _t5_bucket_np(n):
    n = np.maximum(n, 0)
    nf = np.maximum(n, 1).astype(np.float32)
    large = 16 + (np.log(nf / np.float32(16)) / np.float32(math.log(8.0)) * np.float32(16)).astype(np.int32)
    large = np.minimum(large, 31)
    return np.where(n < 16, n, large)


def host_consts():
    ident = np.eye(128, dtype=np.float32).astype(ml_dtypes.bfloat16)
    J = np.ascontiguousarray(np.eye(128, dtype=np.float32)[::-1])
    oh = np.zeros((2, 33, 383), np.float32)
    for r in range(383):
        rel = r - 127
        if rel < 0:
            oh[0, 32, r] = 1.0
            oh[1, 32, r] = 1.0
        else:
            b = int(_t5_bucket_np(np.array([rel]))[0])
            oh[0, b, r] = 1.0
            if rel < 128:
                oh[1, b, r] = 1.0
            else:
                oh[1, 32, r] = 1.0
    half = 16
    invf = np.power(np.float32(10000.0), -(np.arange(half, dtype=np.float32) / np.float32(half))).astype(np.float32)
    invf = np.ascontiguousarray(np.broadcast_to(invf[None, :], (128, 16)))
    return dict(ident=ident, J=J, oh=oh, invf=invf)


def prep_inputs(inputs, cores=range(8)):
    c = host_consts()
    f = lambda a: np.ascontiguousarray(np.asarray(a))
    shared = dict(c)
    shared["rel_bias"] = f(inputs["rel_bias"])
    for k in ("norm_pre", "norm_post", "w_in", "da_subln", "mla_q_norm", "mla_w_qb", "mla_kv_norm",
              "mla_w_kvb", "sw_sinks", "w_br_a", "w_br_b", "w_br_c", "w_out", "w_ple_gate", "w_ple_proj"):
        shared[k] = f(inputs[k])
    shared["da_lambda"] = f(np.asarray(inputs["da_lambda"]).reshape(2, 256))
    x = np.asarray(inputs["x"]); p = np.asarray(inputs["p"]); pos = np.asarray(inputs["positions"])
    maps = []
    for b in cores:
        m = dict(shared)
        m["x"] = f(x[b])
        m["pT"] = f(np.transpose(p[:, b], (0, 2, 1)))
        m["pos"] = f(pos[b].reshape(16, 128).T.astype(np.int32))
        maps.append(m)
    return maps


_NC_CACHE = {}


def kernel(**inputs):
    if "nc" not in _NC_CACHE:
        _NC_CACHE["nc"] = build(nl=2)
    nc = _NC_CACHE["nc"]
    maps = prep_inputs(inputs)
    res = run_bass_kernel_spmd(nc, maps, core_ids=list(range(8)))
    out = np.stack([np.asarray(r["out"]) for r in res.results], axis=0)
    return out.astype(np.float32)
```
